# Optimizing a Trainium2 kernel written in Bass

```python
import jax
import jax.numpy as jnp
from jax import lax
import numpy as np

D_MODEL = 2048
BATCH = 2
SEQ = 4096
DEPTH = 1
DEC_BATCH = 128
DEC_SEQ = 8
PAST_LEN = 16384
PAGE_SIZE = 128

WINDOW = 128
ATT_HEADS = 16
ATT_KV_HEADS = 4
ATT_HEAD_DIM = 64
ATT_GROUP = ATT_HEADS // ATT_KV_HEADS
ATT_Q = ATT_HEADS * ATT_HEAD_DIM
ATT_KV = ATT_KV_HEADS * ATT_HEAD_DIM
RET_HEADS = 8
RET_DK = 128
RET_DV = 256
RET_CHUNK = 128
RET_QK = RET_HEADS * RET_DK
RET_V = RET_HEADS * RET_DV
ROPE_BASE = 10000.0
D_FF = 5632
NORM_EPS = 1e-6
N_SUBLAYERS = 3
IN_SPLITS = (ATT_Q, ATT_KV, ATT_KV, RET_QK, RET_QK, RET_V, RET_V, D_MODEL, D_MODEL)
D_IN = ATT_Q + 2 * ATT_KV + 2 * RET_QK + 2 * RET_V + 2 * D_MODEL

kernel_name = 'hybrid_swa_retention_macaron_step'


def rms_norm(x, gain=None):
    xf = x.astype(jnp.float32)
    y = xf * lax.rsqrt(jnp.mean(xf * xf, axis=-1, keepdims=True) + NORM_EPS)
    if gain is not None:
        y = y * gain.astype(jnp.float32)
    return y.astype(x.dtype)


def swiglu(h, wg, wu, wd):
    return (jax.nn.silu(h @ wg) * (h @ wu)) @ wd


def split_cols(proj):
    parts = []
    start = 0
    for width in IN_SPLITS:
        parts.append(proj[..., start:start + width])
        start += width
    return parts


def rotate(x, pos):
    half = x.shape[-1] // 2
    inv_freq = ROPE_BASE ** (-jnp.linspace(0.0, 1.0, half, dtype=jnp.float32))
    ang = pos[:, None] * inv_freq[None, :]
    cos = jnp.cos(ang)[None, :, None, :]
    sin = jnp.sin(ang)[None, :, None, :]
    x1, x2 = x[..., :half], x[..., half:]
    return jnp.concatenate([x1 * cos - x2 * sin, x1 * sin + x2 * cos], axis=-1)


def alibi_slopes():
    return 2.0 ** (-8.0 * jnp.arange(1, ATT_HEADS + 1, dtype=jnp.float32) / ATT_HEADS)


def window_attend(q, k, v, kvalid, q_off, sinks):
    n, tq = q.shape[0], q.shape[1]
    tk = k.shape[1]
    qg = q.astype(jnp.float32).reshape(n, tq, ATT_KV_HEADS, ATT_GROUP, ATT_HEAD_DIM)
    s = jnp.einsum('nqkgd,nskd->nkgqs', qg, k.astype(jnp.float32)) * (ATT_HEAD_DIM ** -0.5)
    dist = q_off + jnp.arange(tq)[:, None] - jnp.arange(tk)[None, :]
    slopes = alibi_slopes().reshape(ATT_KV_HEADS, ATT_GROUP, 1, 1)
    s = s - slopes * dist.astype(jnp.float32)
    mask = ((dist >= 0) & (dist <= WINDOW))[None, None, None] & kvalid[:, None, None, None, :]
    s = jnp.where(mask, s, -jnp.inf)
    sink = sinks.astype(jnp.float32).reshape(1, ATT_KV_HEADS, ATT_GROUP, 1, 1)
    m = jnp.maximum(jnp.max(s, axis=-1, keepdims=True), sink)
    p = jnp.exp(s - m)
    p = p / (jnp.sum(p, axis=-1, keepdims=True) + jnp.exp(sink - m))
    o = jnp.einsum('nkgqs,nskd->nqkgd', p, v.astype(jnp.float32))
    return o.reshape(n, tq, ATT_Q).astype(q.dtype)


def banded_prompt_attention(q, k, v, sinks):
    b, s_len = q.shape[0], q.shape[1]
    nb = s_len // WINDOW
    pad = jnp.zeros((b, WINDOW) + k.shape[2:], k.dtype)
    kb = jnp.concatenate([pad, k], axis=1).reshape(b, nb + 1, WINDOW, ATT_KV_HEADS, ATT_HEAD_DIM)
    vb = jnp.concatenate([pad, v], axis=1).reshape(b, nb + 1, WINDOW, ATT_KV_HEADS, ATT_HEAD_DIM)
    kblk = jnp.concatenate([kb[:, :-1], kb[:, 1:]], axis=2).reshape(b * nb, 2 * WINDOW, ATT_KV_HEADS, ATT_HEAD_DIM)
    vblk = jnp.concatenate([vb[:, :-1], vb[:, 1:]], axis=2).reshape(b * nb, 2 * WINDOW, ATT_KV_HEADS, ATT_HEAD_DIM)
    qblk = q.reshape(b * nb, WINDOW, ATT_HEADS, ATT_HEAD_DIM)
    key_pos = jnp.arange(nb)[:, None] * WINDOW - WINDOW + jnp.arange(2 * WINDOW)[None, :]
    kvalid = jnp.broadcast_to((key_pos >= 0)[None], (b, nb, 2 * WINDOW)).reshape(b * nb, 2 * WINDOW)
    o = window_attend(qblk, kblk, vblk, kvalid, WINDOW, sinks)
    return o.reshape(b, s_len, ATT_Q)


def retention(q, k, v, s0, chunk):
    n, t, h, dk = q.shape
    dv = v.shape[-1]
    nc = t // chunk
    log_g = jnp.log(1.0 - 2.0 ** (-5.0 - jnp.arange(h, dtype=jnp.float32)))
    idx = jnp.arange(chunk, dtype=jnp.float32)
    diff = idx[:, None] - idx[None, :]
    decay = jnp.where(diff >= 0, jnp.exp(jnp.maximum(diff, 0.0)[None] * log_g[:, None, None]), 0.0)
    qc = q.reshape(n, nc, chunk, h, dk)
    kc = k.reshape(n, nc, chunk, h, dk)
    vc = v.reshape(n, nc, chunk, h, dv)
    scores = jnp.einsum('ncihd,ncjhd->nchij', qc, kc) * decay
    y_in = jnp.einsum('nchij,ncjhe->ncihe', scores, vc)
    k_w = jnp.exp((chunk - 1.0 - idx)[:, None] * log_g[None, :])
    kv = jnp.einsum('ncjhd,ncjhe->nchde', kc * k_w[:, :, None], vc)
    g_chunk = jnp.exp(chunk * log_g)[:, None, None]

    def step(s, kv_i):
        return g_chunk * s + kv_i, s

    s_fin, s_prev = lax.scan(step, s0, jnp.moveaxis(kv, 1, 0))
    q_w = jnp.exp((idx + 1.0)[:, None] * log_g[None, :])
    y_cross = jnp.einsum('ncihd,cnhde->ncihe', qc * q_w[:, :, None], s_prev)
    return (y_in + y_cross).reshape(n, t, h, dv), s_fin


def decoder_layer(x, c, pos, k_past, v_past, s_past, w_ada, b_ada, norm_pre, norm_post, w_in, sinks,
                  w_pa, w_pr, w_o, f1g, f1u, f1d, f2g, f2u, f2d):
    n, t, _ = x.shape
    mod = (jax.nn.silu(c) @ w_ada + b_ada).reshape(n, N_SUBLAYERS, 3, D_MODEL)
    shift, scale, gate = mod[:, :, 0], mod[:, :, 1], mod[:, :, 2]

    def pre(i, y):
        return rms_norm(y, norm_pre[i]) * (1.0 + scale[:, i, None]) + shift[:, i, None]

    def post(i, y):
        return gate[:, i, None] * rms_norm(y, norm_post[i])

    x = x + 0.5 * post(0, swiglu(pre(0, x), f1g, f1u, f1d))

    h = pre(1, x)
    qa, ka, va, qr, kr, vr, gr, gate_a, gate_r = split_cols(h @ w_in)
    qa = qa.reshape(n, t, ATT_HEADS, ATT_HEAD_DIM)
    ka = ka.reshape(n, t, ATT_KV_HEADS, ATT_HEAD_DIM)
    va = va.reshape(n, t, ATT_KV_HEADS, ATT_HEAD_DIM)
    if k_past is None:
        o_a = banded_prompt_attention(qa, ka, va, sinks)
        k_new, v_new = ka[:, -WINDOW:], va[:, -WINDOW:]
        s0 = jnp.zeros((n, RET_HEADS, RET_DK, RET_DV), jnp.float32)
        chunk = RET_CHUNK
    else:
        kcat = jnp.concatenate([k_past.astype(ka.dtype), ka], axis=1)
        vcat = jnp.concatenate([v_past.astype(va.dtype), va], axis=1)
        o_a = window_attend(qa, kcat, vcat, jnp.ones((n, kcat.shape[1]), bool), WINDOW, sinks)
        k_new, v_new = kcat[:, -WINDOW:], vcat[:, -WINDOW:]
        s0 = s_past.astype(jnp.float32)
        chunk = t
    qr = rotate(qr.reshape(n, t, RET_HEADS, RET_DK).astype(jnp.float32), pos)
    kr = rotate(kr.reshape(n, t, RET_HEADS, RET_DK).astype(jnp.float32), pos) * (RET_DK ** -0.5)
    vr = vr.reshape(n, t, RET_HEADS, RET_DV).astype(jnp.float32)
    o_r, s_new = retention(qr, kr, vr, s0, chunk)
    o_r = rms_norm(o_r).reshape(n, t, RET_V).astype(h.dtype)
    o_r = jax.nn.silu(gr) * o_r
    merged = jax.nn.sigmoid(gate_a) * (o_a @ w_pa) + jax.nn.sigmoid(gate_r) * (o_r @ w_pr)
    x = x + post(1, merged @ w_o)

    x = x + 0.5 * post(2, swiglu(pre(2, x), f2g, f2u, f2d))
    return x, k_new, v_new, s_new.astype(x.dtype)


def setup_inputs(seed: int = 0) -> dict:
    key = jax.random.key(seed)
    ks = jax.random.split(key, 22)
    f32 = jnp.float32
    nrm = lambda k, shape, s: jax.random.normal(k, shape, f32) * s
    return {
        'x_prompt': nrm(ks[0], (BATCH, SEQ, D_MODEL), 1.0),
        'x_sample': nrm(ks[1], (DEC_BATCH, DEC_SEQ, D_MODEL), 1.0),
        'cache_k_win': nrm(ks[2], (DEPTH, DEC_BATCH, WINDOW, ATT_KV_HEADS, ATT_HEAD_DIM), 1.0),
        'cache_v_win': nrm(ks[3], (DEPTH, DEC_BATCH, WINDOW, ATT_KV_HEADS, ATT_HEAD_DIM), 1.0),
        'state_ret': nrm(ks[4], (DEPTH, DEC_BATCH, RET_HEADS, RET_DK, RET_DV), 0.5),
        'c_prompt': nrm(ks[5], (BATCH, D_MODEL), 1.0),
        'c_sample': nrm(ks[6], (DEC_BATCH, D_MODEL), 1.0),
        'w_ada': nrm(ks[7], (DEPTH, D_MODEL, N_SUBLAYERS * 3 * D_MODEL), 0.5 * D_MODEL ** -0.5),
        'b_ada': nrm(ks[8], (DEPTH, N_SUBLAYERS * 3 * D_MODEL), 0.02),
        'norm_pre': 1.0 + nrm(ks[9], (DEPTH, N_SUBLAYERS, D_MODEL), 0.05),
        'norm_post': 1.0 + nrm(ks[10], (DEPTH, N_SUBLAYERS, D_MODEL), 0.05),
        'w_in': nrm(ks[11], (DEPTH, D_MODEL, D_IN), D_MODEL ** -0.5),
        'attn_sinks': nrm(ks[12], (DEPTH, ATT_HEADS), 0.5),
        'w_pa': nrm(ks[13], (DEPTH, ATT_Q, D_MODEL), ATT_Q ** -0.5),
        'w_pr': nrm(ks[14], (DEPTH, RET_V, D_MODEL), RET_V ** -0.5),
        'w_o': nrm(ks[15], (DEPTH, D_MODEL, D_MODEL), D_MODEL ** -0.5),
        'ffn1_gate': nrm(ks[16], (DEPTH, D_MODEL, D_FF), D_MODEL ** -0.5),
        'ffn1_up': nrm(ks[17], (DEPTH, D_MODEL, D_FF), D_MODEL ** -0.5),
        'ffn1_down': nrm(ks[18], (DEPTH, D_FF, D_MODEL), D_FF ** -0.5),
        'ffn2_gate': nrm(ks[19], (DEPTH, D_MODEL, D_FF), D_MODEL ** -0.5),
        'ffn2_up': nrm(ks[20], (DEPTH, D_MODEL, D_FF), D_MODEL ** -0.5),
        'ffn2_down': nrm(ks[21], (DEPTH, D_FF, D_MODEL), D_FF ** -0.5),
    }


def reference(x_prompt, x_sample, cache_k_win, cache_v_win, state_ret, c_prompt, c_sample,
              w_ada, b_ada, norm_pre, norm_post, w_in, attn_sinks, w_pa, w_pr, w_o,
              ffn1_gate, ffn1_up, ffn1_down, ffn2_gate, ffn2_up, ffn2_down):
    pos_p = jnp.arange(x_prompt.shape[1], dtype=jnp.float32)
    pos_s = jnp.arange(x_sample.shape[1], dtype=jnp.float32) + PAST_LEN
    yp, ys = x_prompt, x_sample
    kp_l, vp_l, sp_l, ks_l, vs_l, ss_l = [], [], [], [], [], []
    for l in range(DEPTH):
        w = (w_ada[l], b_ada[l], norm_pre[l], norm_post[l], w_in[l], attn_sinks[l], w_pa[l], w_pr[l],
             w_o[l], ffn1_gate[l], ffn1_up[l], ffn1_down[l], ffn2_gate[l], ffn2_up[l], ffn2_down[l])
        yp, kp, vp, sp = decoder_layer(yp, c_prompt, pos_p, None, None, None, *w)
        ys, kss, vss, sss = decoder_layer(ys, c_sample, pos_s, cache_k_win[l], cache_v_win[l], state_ret[l], *w)
        kp_l.append(kp)
        vp_l.append(vp)
        sp_l.append(sp)
        ks_l.append(kss)
        vs_l.append(vss)
        ss_l.append(sss)
    k_win_prompt = jnp.stack(kp_l)
    v_win_prompt = jnp.stack(vp_l)
    state_ret_prompt = jnp.stack(sp_l)
    k_win_sample = jnp.stack(ks_l)
    v_win_sample = jnp.stack(vs_l)
    state_ret_sample = jnp.stack(ss_l)
    return (yp, ys, k_win_prompt, v_win_prompt, state_ret_prompt, k_win_sample, v_win_sample, state_ret_sample)
```

```python
import contextlib
import numpy as np
import concourse.bass as bass
import concourse.mybir as mybir
from concourse.bass_utils import run_bass_kernel_spmd

F32 = mybir.dt.float32
BF16 = mybir.dt.bfloat16
AF = mybir.ActivationFunctionType
ALU = mybir.AluOpType
AX = mybir.AxisListType

D = 2048
KD = 16
DFF = 5632
NFB = 44
SEQ = 4096
NPASS = 4
PT = 1024
NB = 16
DIN = 11776
EPS = 1e-6
BIG = 40000.0
ARENA_WORDS = 47500
import os as _os
SWAP_ENG = _os.environ.get('SWAP_ENG', 'pool')
MIXSKIP = _os.environ.get('MIXSKIP', '')


class Buf:
    __slots__ = ("name", "w", "r", "excl")

    def __init__(self, name=""):
        self.name = name
        self.w = None
        self.r = []
        self.excl = False


class Op:
    __slots__ = ("eng", "fn", "deps", "idx", "count", "lane", "needed", "sem", "inc")


class Sched:
    ENGS = ("pe", "act", "dve", "pool", "sp")

    def __init__(self, nc):
        self.nc = nc
        self.ops = {e: [] for e in self.ENGS}
        self.all_dma = []
        self.lane_slot = {}
        self.next_slot = {}
        self.slot_last = {}

    def stage_begin(self):
        self.lane_slot = {}
        self.next_slot = {}

    def add(self, eng, fn, reads=(), writes=(), lane=None, inc=16):
        op = Op()
        op.eng, op.fn, op.lane, op.inc = eng, fn, lane, inc
        op.needed, op.sem, op.count = False, None, 0
        writes = list(writes) + [b for b in reads if b.excl and b not in writes]
        reads = [b for b in reads if not b.excl]
        deps = set()
        for b in reads:
            if b.w is not None:
                deps.add(b.w)
        for b in writes:
            if b.w is not None:
                deps.add(b.w)
            for r in b.r:
                deps.add(r)
        for b in reads:
            b.r.append(op)
        for b in writes:
            b.w = op
            b.r = []
        if lane is not None:
            if lane not in self.lane_slot:
                kind = (eng, inc)
                n_ = self.next_slot.get(kind, 0)
                self.lane_slot[lane] = (kind, n_)
                self.next_slot[kind] = n_ + 1
            op.lane = self.lane_slot[lane]
            prev = self.slot_last.get(op.lane)
            if prev is not None:
                deps.add(prev)
            self.slot_last[op.lane] = op
        deps.discard(op)
        last = {}
        keep = []
        for d in deps:
            if d.lane is not None:
                keep.append(d)
            elif d.eng not in last or d.idx > last[d.eng].idx:
                last[d.eng] = d
        op.deps = keep + list(last.values())
        op.idx = len(self.ops[eng])
        self.ops[eng].append(op)
        if lane is not None:
            self.all_dma.append(op)
        return op

    EPOCH = 30000

    def emit(self):
        nc = self.nc
        for e in self.ENGS:
            for op in self.ops[e]:
                for d in op.deps:
                    if d.lane is not None or d.eng != op.eng:
                        d.needed = True
                    elif op.eng != "pe" and d.idx >= op.idx - 2:
                        d.needed = True
        lane_names = []
        for op in self.all_dma:
            if op.lane not in lane_names:
                lane_names.append(op.lane)
        nsem = 0
        for e in self.ENGS:
            slot, c = nsem, 0
            nsem += 1
            for op in self.ops[e]:
                if op.lane is None and op.needed:
                    if c >= self.EPOCH:
                        slot, c = nsem, 0
                        nsem += 1
                    c += 1
                    op.count, op.sem = c, slot
        lstate = {}
        for op in self.all_dma:
            if op.lane not in lstate:
                lstate[op.lane] = [nsem, 0]
                nsem += 1
            stt = lstate[op.lane]
            if stt[1] + op.inc > self.EPOCH:
                stt[0], stt[1] = nsem, 0
                nsem += 1
            stt[1] += op.inc
            op.count, op.sem = stt[1], stt[0]
        final = {}
        for op in self.all_dma:
            final[op.sem] = max(final.get(op.sem, 0), op.count)
        print("semaphores used: %d" % nsem)
        with contextlib.ExitStack() as st:
            sems = [st.enter_context(nc.semaphore("s%d" % i)) for i in range(nsem)]
            block = st.enter_context(nc.Block())
            hw = {"pe": block.tensor, "act": block.scalar, "dve": block.vector, "pool": block.gpsimd,
                  "sp": block.sync}

            def make(e):
                def body(eng):
                    waited = {}
                    for op in self.ops[e]:
                        for d in op.deps:
                            if d.lane is None and d.eng == e and (e == "pe" or d.idx < op.idx - 2):
                                continue
                            if waited.get(d.sem, 0) >= d.count:
                                continue
                            eng.wait_ge(sems[d.sem], d.count)
                            waited[d.sem] = d.count
                        ins = op.fn(eng)
                        if op.lane is not None:
                            ins.then_inc(sems[op.sem], op.inc)
                        elif op.needed:
                            ins.then_inc(sems[op.sem], 1)
                    if e == "sp":
                        for sidx, cnt in final.items():
                            if waited.get(sidx, 0) < cnt:
                                eng.wait_ge(sems[sidx], cnt)
                return body

            for e in self.ENGS:
                hw[e](make(e))


class Tile:
    def __init__(self, ar, off, shape, dt, name):
        self.ar, self.off, self.shape, self.dt, self.name = ar, off, list(shape), dt, name
        self.es = 4 if dt == F32 else 2
        self.b = Buf(name)
        self.extra = []
        st = []
        s = 1
        for n in reversed(self.shape):
            st.append(s)
            s *= n
        self.strides = list(reversed(st))
        self.nbytes = s * self.es

    def ap(self, idx=None, p0=0, pn=128, merge=True):
        h = self.ar.h32 if self.dt == F32 else self.ar.h16
        row = h.shape[1]
        off = self.off // self.es
        dims = []
        idx = idx or [None] * len(self.shape)
        idx = list(idx) + [None] * (len(self.shape) - len(idx))
        for i, (ix, n, s) in enumerate(zip(idx, self.shape, self.strides)):
            if ix is None:
                dims.append([s, n])
            elif isinstance(ix, int):
                off += ix * s
            else:
                st_, cnt = ix[0], ix[1]
                step = ix[2] if len(ix) > 2 else 1
                off += st_ * s
                dims.append([s * step, cnt])
        if not dims:
            dims = [[1, 1]]
        m = [dims[0]]
        for dd in dims[1:]:
            if merge and m[-1][0] == dd[0] * dd[1]:
                m[-1] = [dd[0], m[-1][1] * dd[1]]
            else:
                m.append(dd)
        return bass.AP(h, p0 * row + off, [[row, pn]] + m)


class Arena:
    def __init__(self, nc, st, words):
        self.t = st.enter_context(nc.sbuf_tensor("arena", [128, words], F32))
        self.h32 = self.t
        self.h16 = self.t.bitcast(BF16)
        self.size = words * 4
        self.top = 0
        self.live = []
        self.dead = []
        self.hw = 0

    def alloc(self, name, shape, dt):
        t = Tile(self, self.top, shape, dt, name)
        sz = (t.nbytes + 31) // 32 * 32
        assert self.top + sz <= self.size, "arena overflow at %s: %d" % (name, self.top + sz)
        s, e = self.top, self.top + sz
        self.top = e
        self.hw = max(self.hw, e)
        for (ds, de, dt_) in self.dead:
            if ds < e and s < de:
                for db in [dt_.b] + dt_.extra:
                    t.b.r.extend(db.r)
                    if db.w is not None:
                        t.b.r.append(db.w)
        self.live.append((s, e, t))
        return t

    def mark(self):
        return self.top

    def release(self, mark):
        keep = []
        for ent in self.live:
            if ent[0] >= mark:
                self.dead.append(ent)
            else:
                keep.append(ent)
        self.live = keep
        self.top = mark


def _tables():
    f = np.float32
    t = {}
    t["identf"] = np.eye(128, dtype=f)
    half = 64
    inv = (10000.0 ** (-np.linspace(0.0, 1.0, half, dtype=f))).astype(f)
    pos = np.zeros((33, 128), f)
    for i in range(32):
        pos[i] = np.arange(128, dtype=f) + f(i * 128)
    pos[32] = (np.arange(128) % 8).astype(f) + f(16384.0)
    ang = (pos[:, :, None].astype(f) * inv[None, None, :]).astype(f)
    cos = np.cos(ang).astype(f)
    sin = np.sin(ang).astype(f)
    sc = f(128.0 ** -0.5)
    rotc = np.concatenate([cos, cos, cos * sc, cos * sc], -1)
    rots = np.concatenate([-sin, sin, -sin * sc, sin * sc], -1)
    t["rotc"] = np.ascontiguousarray(rotc.transpose(1, 0, 2)).astype(f)
    t["rots"] = np.ascontiguousarray(rots.transpose(1, 0, 2)).astype(f)
    h = np.arange(8, dtype=np.float64)
    logg = np.log(1.0 - 2.0 ** (-5.0 - h))
    i = np.arange(128)
    diff = i[:, None] - i[None, :]
    dec = np.where(diff >= 0, np.exp(np.maximum(diff, 0)[None] * logg[:, None, None]), 0.0)
    t["decT_p"] = np.ascontiguousarray(dec.transpose(2, 0, 1)).astype(f)
    b_of = i // 8
    t_of = i % 8
    same = (b_of[:, None] == b_of[None, :])
    dsm = t_of[:, None] - t_of[None, :]
    decs = np.where(same[None] & (dsm >= 0)[None], np.exp(np.maximum(dsm, 0)[None] * logg[:, None, None]), 0.0)
    t["decT_s"] = np.ascontiguousarray(decs.transpose(2, 0, 1)).astype(f)
    qw_p = np.exp((i + 1.0)[None, :] * logg[:, None])
    qw_s = np.exp((t_of + 1.0)[None, :] * logg[:, None])
    t["qw_p"] = np.ascontiguousarray(np.broadcast_to(qw_p[None], (128, 8, 128))).astype(f)
    t["qw_s"] = np.ascontiguousarray(np.broadcast_to(qw_s[None], (128, 8, 128))).astype(f)
    t["kw_p"] = np.exp((127.0 - i)[:, None] * logg[None, :]).astype(f)
    t["kw_s"] = np.exp((7.0 - t_of)[:, None] * logg[None, :]).astype(f)
    t["g128"] = np.exp(128.0 * logg)
    t["g8"] = np.exp(8.0 * logg)
    q = np.arange(128)[:, None]
    kk = np.arange(256)[None, :]
    dist = 128 + q - kk
    valid = (dist >= 0) & (dist <= 128)
    t["dgen"] = np.where(valid, dist, BIG).astype(f)
    vf = valid & (kk >= 128)
    t["dfirst"] = np.where(vf, dist, BIG).astype(f)
    ds = np.full((128, 2176), BIG, f)
    for r in range(128):
        b, tt = r // 8, r % 8
        j = np.arange(128)
        dd = 128 + tt - j
        ok = (dd >= 0) & (dd <= 128)
        ds[r, b * 128:(b + 1) * 128] = np.where(ok, dd, BIG)
        for t2 in range(8):
            d2 = tt - t2
            if d2 >= 0:
                ds[r, 2048 + b * 8 + t2] = d2
    t["dsamp"] = ds
    t["slopes"] = 2.0 ** (-8.0 * np.arange(1, 17, dtype=np.float64) / 16.0)
    t["bmask"] = (b_of[:, None] == np.arange(16)[None, :]).astype(f)
    return t


TAB = _tables()


class PsBank:
    def __init__(self, t):
        self.h32 = t
        self.h16 = t.bitcast(BF16)


class _StopBuild(Exception):
    pass


def build(debug=None, npass=NPASS, stop_after=None, fake_weights=False):
    nc = bass.Bass("TRN2", target_bir_lowering=False)
    S = Sched(nc)
    big = {"w_ada", "w_in", "w_pa", "w_pr", "w_o", "f1g", "f1u", "f1d", "f2g", "f2u", "f2d"}

    def din(name, shape):
        if fake_weights and name in big:
            return nc.dram_tensor(name, list(shape), F32).ap()
        return nc.dram_tensor(name, list(shape), F32, kind="ExternalInput").ap()

    def stop_here(tag):
        if stop_after == tag:
            raise _StopBuild()

    def dout(name, shape):
        return nc.dram_tensor(name, list(shape), F32, kind="ExternalOutput").ap()

    xp = din("xp", [SEQ, D]); xs = din("xs", [128, D]); cT = din("cT", [D, 17])
    w_ada = din("w_ada", [D, 9 * D]); badd = din("badd", [128, 144 * 17]); normsT = din("normsT", [128, 96])
    sinksR = din("sinksR", [128, 16])
    w_in = din("w_in", [D, DIN]); w_pa = din("w_pa", [1024, D]); w_pr = din("w_pr", [D, D]); w_o = din("w_o", [D, D])
    f1g = din("f1g", [D, DFF]); f1u = din("f1u", [D, DFF]); f1d = din("f1d", [DFF, D])
    f2g = din("f2g", [D, DFF]); f2u = din("f2u", [D, DFF]); f2d = din("f2d", [DFF, D])
    ck = din("ck", [NB, 128, 256]); cv = din("cv", [NB, 128, 256]); st_in = din("st", [NB, 8, 128, 256])
    tb = {k: din("t_" + k, TAB[k].shape) for k in
          ("identf", "rotc", "rots", "decT_p", "decT_s", "qw_p", "qw_s", "kw_p", "kw_s", "dgen", "dfirst", "dsamp", "bmask")}
    yp = dout("yp", [SEQ, D]); ys = dout("ys", [128, D])
    kwp = dout("kwp", [128, 256]); vwp = dout("vwp", [128, 256]); srp = dout("srp", [8, 128, 256])
    kws = dout("kws", [NB, 128, 256]); vws = dout("vws", [NB, 128, 256]); srs = dout("srs", [NB, 8, 128, 256])
    dbg = {}
    if debug:
        for k, shp in debug.items():
            dbg[k] = dout("dbg_" + k, shp)
    xsA = nc.dram_tensor("xsA", [D, 1152], F32).ap()
    xsB = nc.dram_tensor("xsB", [D, 1152], F32).ap()
    xsAb = [Buf() for _ in range(KD)]
    xsBb = [Buf() for _ in range(KD)]

    def W(x):
        return x.b if isinstance(x, Tile) else x

    def A(eng, fn, r=(), w=(), lane=None, inc=16):
        return S.add(eng, fn, [W(x) for x in r], [W(x) for x in w], lane, inc)

    with contextlib.ExitStack() as st:
        ar = Arena(nc, st, ARENA_WORDS)
        pst = [st.enter_context(nc.psum_tensor("ps%d" % i, [128, 512], F32)) for i in range(8)]
        psk = [PsBank(t) for t in pst]
        ps = [Tile(psk[i], 0, [512], F32, "ps%d" % i) for i in range(8)]
        ps16 = [Tile(psk[i], 0, [1024], BF16, "ps16_%d" % i) for i in range(8)]
        ps4 = [Tile(psk[i], 0, [4, 128], F32, "ps4_%d" % i) for i in range(8)]
        for i in range(8):
            ps[i].b.excl = True
            ps16[i].b = ps[i].b
            ps4[i].b = ps[i].b
        lane_ctr = [0]

        def newlane(prefix):
            lane_ctr[0] += 1
            return "%s_%d" % (prefix, lane_ctr[0])

        identf = ar.alloc("identf", [128], F32)
        identb = ar.alloc("identb", [128], BF16)
        onesb = ar.alloc("onesb", [128], BF16)
        cst = ar.alloc("cst", [8], F32)
        modT = ar.alloc("modT", [144, 17], F32)
        norms = ar.alloc("norms", [6, 16], F32)
        sinks = ar.alloc("sinks", [16], F32)
        ApT = ar.alloc("ApT", [3, 16], F32)
        CpT = ar.alloc("CpT", [3, 16], F32)
        Scar = ar.alloc("Scar", [8, 256], F32)
        kTAh = ar.alloc("kTAh", [2, 128], BF16)
        kTBh = ar.alloc("kTBh", [2, 128], BF16)
        vhalo = ar.alloc("vhalo", [256], BF16)
        A("sp", lambda e: e.dma_start(out=identf.ap(), in_=tb["identf"]), w=[identf], lane="c_id")
        A("sp", lambda e: e.dma_start(out=norms.ap(), in_=normsT.rearrange("p (a b) -> p a b", a=6)), w=[norms], lane="c_nm")
        A("sp", lambda e: e.dma_start(out=sinks.ap(), in_=sinksR), w=[sinks], lane="c_sk")
        A("dve", lambda e: e.tensor_copy(out=identb.ap(), in_=identf.ap()), r=[identf], w=[identb])
        A("dve", lambda e: e.memset(onesb.ap(), 1.0), w=[onesb])
        A("dve", lambda e: e.memset(cst.ap([(0, 1)]), EPS), w=[cst])
        A("dve", lambda e: e.memset(cst.ap([(1, 1)]), 0.0), w=[cst])
        A("dve", lambda e: e.memset(Scar.ap(), 0.0), w=[Scar])
        A("dve", lambda e: e.memset(kTAh.ap(), 0.0), w=[kTAh])
        A("dve", lambda e: e.memset(kTBh.ap(), 0.0), w=[kTBh])
        A("dve", lambda e: e.memset(vhalo.ap(), 0.0), w=[vhalo])
        pers_mark = ar.mark()

        def stage_mod():
            S.stage_begin()
            m = ar.mark()
            ct32 = ar.alloc("ct32", [16, 17], F32)
            scT = ar.alloc("scT", [16, 17], BF16)
            bad = ar.alloc("bad", [144, 17], F32)
            wr = [ar.alloc("wada%d" % i, [16, 128], BF16) for i in range(4)]
            A("sp", lambda e: e.dma_start(out=ct32.ap(), in_=cT.rearrange("(k p) n -> p k n", p=128)), w=[ct32], lane="m_ct")
            A("sp", lambda e: e.dma_start(out=bad.ap(), in_=badd.rearrange("p (a b) -> p a b", a=144)), w=[bad], lane="m_bad")
            A("act", lambda e: e.activation(out=scT.ap(), in_=ct32.ap(), func=AF.Silu), r=[ct32], w=[scT])
            for ch in range(144):
                wt = wr[ch % 4]
                src = w_ada[:, ch * 128:(ch + 1) * 128].rearrange("(k p) n -> p k n", p=128)
                A("pool", lambda e, wt=wt, src=src: e.dma_start(out=wt.ap(), in_=src), w=[wt], lane="m_w%d" % (ch % 4))
                bank, j = ch // 30, ch % 30
                for k in range(KD):
                    A("pe", lambda e, wt=wt, k=k, bank=bank, j=j: e.matmul(ps[bank].ap([(j * 17, 17)]), lhsT=wt.ap([k]), rhs=scT.ap([k]),
                                                                       start=(k == 0), stop=(k == KD - 1)),
                      r=[wt, scT], w=[ps[bank]])
            for bank in range(5):
                n = min(30, 144 - bank * 30)
                A("dve", lambda e, bank=bank, n=n: e.tensor_tensor(out=modT.ap([(bank * 30, n)]), in0=ps[bank].ap([(0, n * 17)]),
                                                                   in1=bad.ap([(bank * 30, n)]), op=ALU.add),
                  r=[ps[bank], bad], w=[modT])
            tmp = ar.alloc("mtmp", [16], F32)
            for i in range(3):
                A("dve", lambda e, i=i: e.tensor_scalar(out=tmp.ap(), in0=modT.ap([((3 * i + 1) * 16, 16), 0]), scalar1=1.0, scalar2=None, op0=ALU.add),
                  r=[modT], w=[tmp])
                A("dve", lambda e, i=i: e.tensor_tensor(out=ApT.ap([i]), in0=tmp.ap(), in1=norms.ap([i]), op=ALU.mult), r=[tmp, norms], w=[ApT])
                fct = 1.0 if i == 1 else 0.5
                A("dve", lambda e, i=i, fct=fct: e.scalar_tensor_tensor(out=CpT.ap([i]), in0=modT.ap([((3 * i + 2) * 16, 16), 0]), scalar=fct,
                                                                        in1=norms.ap([3 + i]), op0=ALU.mult, op1=ALU.mult),
                  r=[modT, norms], w=[CpT])
            ar.release(m)

        def samp_table(dst, i, kind):
            m = ar.mark()
            t16 = ar.alloc("st16", [16, 16], F32)
            if kind == "A":
                A("pool", lambda e: e.tensor_scalar(out=t16.ap(), in0=modT.ap([((3 * i + 1) * 16, 16), (1, 16)]), scalar1=1.0, scalar2=None, op0=ALU.add),
                  r=[modT], w=[t16])
                nb = bass.AP(ar.h32, norms.off // 4 + i * 16, [[ar.h32.shape[1], 128], [1, 16], [0, 16]])
                A("pool", lambda e: e.tensor_tensor(out=t16.ap(), in0=t16.ap(), in1=nb, op=ALU.mult), r=[t16, norms], w=[t16])
                src = t16.ap()
            elif kind == "B":
                src = modT.ap([((3 * i) * 16, 16), (1, 16)])
                t16 = modT
            else:
                fct = 1.0 if i == 1 else 0.5
                nb = bass.AP(ar.h32, norms.off // 4 + (3 + i) * 16, [[ar.h32.shape[1], 128], [1, 16], [0, 16]])
                A("pool", lambda e: e.scalar_tensor_tensor(out=t16.ap(), in0=modT.ap([((3 * i + 2) * 16, 16), (1, 16)]), scalar=fct, in1=nb,
                                                           op0=ALU.mult, op1=ALU.mult) if False else
                  e.tensor_tensor(out=t16.ap(), in0=modT.ap([((3 * i + 2) * 16, 16), (1, 16)]), in1=nb, op=ALU.mult),
                  r=[modT, norms], w=[t16])
                A("pool", lambda e: e.tensor_scalar(out=t16.ap(), in0=t16.ap(), scalar1=fct, scalar2=None, op0=ALU.mult), r=[t16], w=[t16])
                src = t16.ap()
            for t in range(8):
                A("pool", lambda e, t=t, src=src: e.tensor_copy(out=dst.ap([None, (t, 16, 8)]), in_=src), r=[t16], w=[dst])
            ar.release(m)

        def stats(acc, accb, T, groups, rs, sq):
            for k in range(KD):
                sqt = sq[k % 2]
                A("act", lambda e, k=k, sqt=sqt: e.activation(out=sqt.ap([(0, T)]), in_=acc.ap([k, (0, T)]), func=AF.Square),
                  r=[accb[k][g] for g in range(len(groups))], w=[sqt])
                for g, (g0, gw) in enumerate(groups):
                    A("pe", lambda e, k=k, sqt=sqt, g=g, g0=g0, gw=gw: e.matmul(ps[5 + g].ap([(0, gw)]), lhsT=onesb.ap(), rhs=sqt.ap([(g0, gw)]),
                                                                                start=(k == 0), stop=(k == KD - 1)),
                      r=[sqt, onesb], w=[ps[5 + g]])
            for g, (g0, gw) in enumerate(groups):
                A("act", lambda e, g=g, g0=g0, gw=gw: e.activation(out=rs.ap([(g0, gw)]), in_=ps[5 + g].ap([(0, gw)]), func=AF.Sqrt,
                                                                   scale=1.0 / D, bias=cst.ap([(0, 1)])),
                  r=[ps[5 + g], cst], w=[rs])
            A("dve", lambda e: e.reciprocal(out=rs.ap([(0, T)]), in_=rs.ap([(0, T)])), r=[rs], w=[rs])

        def normapply(i, acc, accb, hT, hTb, T, groups, rs, tmp, As, Bs):
            samp = T > PT
            for k in range(KD):
                tt = tmp[k % 2]
                A("dve", lambda e, k=k, tt=tt: e.tensor_tensor(out=tt.ap([(0, T)]), in0=acc.ap([k, (0, T)]), in1=rs.ap([(0, T)]), op=ALU.mult),
                  r=[accb[k][g] for g in range(len(groups))] + [rs], w=[tt])
                A("act", lambda e, k=k, tt=tt: e.activation(out=hT.ap([k, (0, PT)]), in_=tt.ap([(0, PT)]), func=AF.Identity,
                                                            scale=ApT.ap([i, (k, 1)]), bias=modT.ap([3 * i * 16 + k, (0, 1)])),
                  r=[tt, ApT, modT], w=[hTb[0], hTb[1]])
                if samp:
                    A("pool", lambda e, k=k, tt=tt: e.tensor_tensor(out=tt.ap([(PT, 128)]), in0=tt.ap([(PT, 128)]), in1=As.ap([k]), op=ALU.mult),
                      r=[tt, As], w=[tt])
                    A("pool", lambda e, k=k, tt=tt: e.tensor_tensor(out=hT.ap([k, (PT, 128)]), in0=tt.ap([(PT, 128)]), in1=Bs.ap([k]), op=ALU.add),
                      r=[tt, Bs], w=[hTb[2]])

        def stage_p0(p, NT, T, groups, acc, accb, hT, hTb, stash, stashb):
            S.stage_begin()
            m = ar.mark()
            xt = [ar.alloc("xt%d" % i, [D], F32) for i in range(2)]
            for i in range(NT):
                src = xp[p * PT + i * 128: p * PT + (i + 1) * 128, :] if i < 8 else xs[:, :]
                t_ = xt[i % 2]
                A("sp", lambda e, t_=t_, src=src: e.dma_start(out=t_.ap(), in_=src), w=[t_], lane="xt%d" % (i % 2))
                g = min(i // 4, 2)
                for kq in range(4):
                    bank = (i * 4 + kq) % 4
                    for j in range(4):
                        k = kq * 4 + j
                        A("pe", lambda e, t_=t_, k=k, bank=bank, j=j: e.matmul(ps[bank].ap([(j * 128, 128)]), lhsT=t_.ap([(k * 128, 128)]),
                                                                               rhs=identf.ap(), start=True, stop=True),
                          r=[t_, identf], w=[ps[bank]])
                    eng = "act" if kq % 2 == 0 else "dve"
                    wr_ = [accb[kq * 4 + j][g] for j in range(4)]
                    if eng == "act":
                        A("act", lambda e, bank=bank, kq=kq, i=i: e.copy(out=acc.ap([(kq * 4, 4), (i * 128, 128)]), in_=ps4[bank].ap(merge=False)),
                          r=[ps[bank]], w=wr_)
                    else:
                        A("dve", lambda e, bank=bank, kq=kq, i=i: e.tensor_copy(out=acc.ap([(kq * 4, 4), (i * 128, 128)]), in_=ps4[bank].ap(merge=False)),
                          r=[ps[bank]], w=wr_)
            ar.release(m)
            stash_out(T, groups, acc, accb, stash, stashb)

        def prenorm(i, p, T, groups, acc, accb, hT, hTb):
            m = ar.mark()
            rs = ar.alloc("rs", [1152], F32)
            sq = [ar.alloc("sq%d" % j, [1152], BF16) for j in range(2)]
            tmp = [ar.alloc("ntmp%d" % j, [1152], F32) for j in range(2)]
            As = Bs = None
            if T > PT:
                As = ar.alloc("As", [16, 128], F32)
                Bs = ar.alloc("Bs", [16, 128], F32)
                samp_table(As, i, "A")
                samp_table(Bs, i, "B")
            stats(acc, accb, T, groups, rs, sq)
            normapply(i, acc, accb, hT, hTb, T, groups, rs, tmp, As, Bs)
            ar.release(m)

        def stage_ffn(wg, wu, wd, T, groups, acc, accb, hT, hTb):
            S.stage_begin()
            m = ar.mark()
            hid = ar.alloc("hid", [4, 1152], BF16)
            hidb = [[Buf() for _ in groups] for _ in range(4)]
            hid.extra = [b_ for row in hidb for b_ in row]
            wgr = [ar.alloc("wg%d" % i, [16, 128], BF16) for i in range(2)]
            wur = [ar.alloc("wu%d" % i, [16, 128], BF16) for i in range(2)]
            wdr = [ar.alloc("wd%d" % i, [D], BF16) for i in range(6)]
            sg = [ar.alloc("sg%d" % i, [512], F32) for i in range(2)]
            cnt = 0
            for G in range(NFB // 4):
                for jj in range(4):
                    j = G * 4 + jj
                    wgt, wut, wdt = wgr[j % 2], wur[j % 2], wdr[j % 6]
                    A("pool", lambda e, wgt=wgt, j=j: e.dma_start(out=wgt.ap(), in_=wg[:, j * 128:(j + 1) * 128].rearrange("(k p) n -> p k n", p=128)),
                      w=[wgt], lane="f_wg%d" % (j % 2))
                    A("pool", lambda e, wut=wut, j=j: e.dma_start(out=wut.ap(), in_=wu[:, j * 128:(j + 1) * 128].rearrange("(k p) n -> p k n", p=128)),
                      w=[wut], lane="f_wu%d" % (j % 2))
                    A("pool", lambda e, wdt=wdt, j=j: e.dma_start(out=wdt.ap(), in_=wd[j * 128:(j + 1) * 128, :]), w=[wdt], lane="f_wd%d" % (j % 6))
                    for g, (g0, gw) in enumerate(groups):
                        bg, bu = cnt % 2, 2 + cnt % 2
                        sgt = sg[cnt % 2]
                        cnt += 1
                        for k in range(KD):
                            A("pe", lambda e, wgt=wgt, k=k, bg=bg, g0=g0, gw=gw: e.matmul(ps[bg].ap([(0, gw)]), lhsT=wgt.ap([k]), rhs=hT.ap([k, (g0, gw)]),
                                                                                          start=(k == 0), stop=(k == KD - 1)),
                              r=[wgt, hTb[g]], w=[ps[bg]])
                        for k in range(KD):
                            A("pe", lambda e, wut=wut, k=k, bu=bu, g0=g0, gw=gw: e.matmul(ps[bu].ap([(0, gw)]), lhsT=wut.ap([k]), rhs=hT.ap([k, (g0, gw)]),
                                                                                          start=(k == 0), stop=(k == KD - 1)),
                              r=[wut, hTb[g]], w=[ps[bu]])
                        A("act", lambda e, sgt=sgt, bg=bg, gw=gw: e.activation(out=sgt.ap([(0, gw)]), in_=ps[bg].ap([(0, gw)]), func=AF.Silu),
                          r=[ps[bg]], w=[sgt])
                        A("dve", lambda e, sgt=sgt, bu=bu, jj=jj, g0=g0, gw=gw: e.tensor_tensor(out=hid.ap([jj, (g0, gw)]), in0=sgt.ap([(0, gw)]),
                                                                                                in1=ps[bu].ap([(0, gw)]), op=ALU.mult),
                          r=[sgt, ps[bu]], w=[hidb[jj][g]])
                dcnt = 0
                for c in range(KD):
                    for g, (g0, gw) in enumerate(groups):
                        bd = 4 + dcnt % 4
                        dcnt += 1
                        for jj in range(4):
                            wdt = wdr[(G * 4 + jj) % 6]
                            A("pe", lambda e, wdt=wdt, jj=jj, c=c, bd=bd, g0=g0, gw=gw: e.matmul(ps[bd].ap([(0, gw)]), lhsT=wdt.ap([(c * 128, 128)]),
                                                                                                 rhs=hid.ap([jj, (g0, gw)]), start=(jj == 0), stop=(jj == 3)),
                              r=[wdt, hidb[jj][g]], w=[ps[bd]])
                        if G == 0:
                            A("act", lambda e, c=c, bd=bd, g0=g0, gw=gw: e.copy(out=acc.ap([c, (g0, gw)]), in_=ps[bd].ap([(0, gw)])),
                              r=[ps[bd]], w=[accb[c][g]])
                        else:
                            A("dve", lambda e, c=c, bd=bd, g0=g0, gw=gw: e.tensor_tensor(out=acc.ap([c, (g0, gw)]), in0=acc.ap([c, (g0, gw)]),
                                                                                         in1=ps[bd].ap([(0, gw)]), op=ALU.add),
                              r=[ps[bd], accb[c][g]], w=[accb[c][g]])
            ar.release(m)

        def stage_post(i, T, groups, acc, accb, stash_in, stash_inb):
            S.stage_begin()
            m = ar.mark()
            rs = ar.alloc("prs", [1152], F32)
            sq = [ar.alloc("psq%d" % j, [1152], BF16) for j in range(2)]
            tmp = [ar.alloc("ptmp%d" % j, [1152], F32) for j in range(2)]
            xo = [ar.alloc("xold%d" % j, [1152], F32) for j in range(2)]
            Cs = None
            if T > PT:
                Cs = ar.alloc("Cs", [16, 128], F32)
                samp_table(Cs, i, "C")
            stats(acc, accb, T, groups, rs, sq)
            ng = len(groups)
            for k in range(KD):
                tt, xt_ = tmp[k % 2], xo[k % 2]
                A("sp", lambda e, k=k, xt_=xt_: e.dma_start(out=xt_.ap([(0, T)]), in_=stash_in[k * 128:(k + 1) * 128, 0:T]),
                  r=[stash_inb[k]], w=[xt_], lane="xold%d" % (k % 2))
                A("dve", lambda e, k=k, tt=tt: e.tensor_tensor(out=tt.ap([(0, T)]), in0=acc.ap([k, (0, T)]), in1=rs.ap([(0, T)]), op=ALU.mult),
                  r=[accb[k][g] for g in range(ng)] + [rs], w=[tt])
                A("dve", lambda e, k=k, tt=tt, xt_=xt_: e.scalar_tensor_tensor(out=acc.ap([k, (0, PT)]), in0=tt.ap([(0, PT)]), scalar=CpT.ap([i, (k, 1)]),
                                                                               in1=xt_.ap([(0, PT)]), op0=ALU.mult, op1=ALU.add),
                  r=[tt, xt_, CpT], w=[accb[k][0], accb[k][1]])
                if T > PT:
                    A("pool", lambda e, k=k, tt=tt: e.tensor_tensor(out=tt.ap([(PT, 128)]), in0=tt.ap([(PT, 128)]), in1=Cs.ap([k]), op=ALU.mult),
                      r=[tt, Cs], w=[tt])
                    A("pool", lambda e, k=k, tt=tt, xt_=xt_: e.tensor_tensor(out=acc.ap([k, (PT, 128)]), in0=tt.ap([(PT, 128)]), in1=xt_.ap([(PT, 128)]), op=ALU.add),
                      r=[tt, xt_], w=[accb[k][2]])
            ar.release(m)

        stash_ser = [Buf() for _ in range(4)]

        def stash_out(T, groups, acc, accb, stash, stashb):
            S.stage_begin()
            for k in range(KD):
                A("sp", lambda e, k=k: e.dma_start(out=stash[k * 128:(k + 1) * 128, 0:T], in_=acc.ap([k, (0, T)])),
                  r=[accb[k][g] for g in range(len(groups))], w=[stashb[k], stash_ser[k % 4]], lane="stash%d" % (k % 4))

        def stage_out(p, NT, T, groups, acc, accb):
            S.stage_begin()
            m = ar.mark()
            yt = [ar.alloc("yt%d" % i, [D], F32) for i in range(2)]
            for i in range(NT):
                t_ = yt[i % 2]
                g = min(i // 4, 2)
                for kq in range(4):
                    bank = (i * 4 + kq) % 4
                    for j in range(4):
                        k = kq * 4 + j
                        A("pe", lambda e, k=k, bank=bank, j=j, i=i: e.matmul(ps[bank].ap([(j * 128, 128)]), lhsT=acc.ap([k, (i * 128, 128)]),
                                                                             rhs=identf.ap(), start=True, stop=True),
                          r=[accb[k][g], identf], w=[ps[bank]])
                    if kq % 2 == 0:
                        A("act", lambda e, bank=bank, kq=kq, t_=t_: e.copy(out=t_.ap([(kq * 512, 512)]), in_=ps[bank].ap()), r=[ps[bank]], w=[t_])
                    else:
                        A("dve", lambda e, bank=bank, kq=kq, t_=t_: e.tensor_copy(out=t_.ap([(kq * 512, 512)]), in_=ps[bank].ap()), r=[ps[bank]], w=[t_])
                dst = yp[p * PT + i * 128: p * PT + (i + 1) * 128, :] if i < 8 else ys[:, :]
                A("sp", lambda e, t_=t_, dst=dst: e.dma_start(out=dst, in_=t_.ap()), r=[t_], lane="yt%d" % (i % 2))
            ar.release(m)


        class SubArena(Arena):
            def __init__(self, parent, tile, bufs):
                self.h32, self.h16 = parent.h32, parent.h16
                self.base = tile.off
                self.size = tile.off + tile.nbytes
                self.top = tile.off
                self.live, self.dead, self.hw = [], [], 0
                ghost = Tile(parent, tile.off, [tile.nbytes // 4], F32, "ghost")
                ghost.extra = list(bufs)
                self.dead.append((tile.off, tile.off + tile.nbytes, ghost))
                self.all_tiles = []

            def alloc(self, name, shape, dt):
                t = Arena.alloc(self, name, shape, dt)
                self.all_tiles.append(t)
                return t

        def half(bank, hf, shape=None, dt=F32):
            return Tile(psk[bank], hf * 1024, shape or ([256] if dt == F32 else [512]), dt, "psh%d_%d" % (bank, hf))

        mS = nc.dram_tensor("mS", [D, 1152], BF16).ap()
        mSb = [Buf() for _ in range(KD)]
        slopes = [float(v) for v in TAB["slopes"]]
        g128 = [float(v) for v in TAB["g128"]]
        g8 = [float(v) for v in TAB["g8"]]
        C_QA, C_KA, C_VA, C_QR, C_KR, C_VR, C_GR, C_GA, C_GRR = 0, 1024, 1280, 1536, 2560, 3584, 5632, 7680, 9728

        def wslice(w, c0, n):
            return w[:, c0:c0 + n].rearrange("(k p) n -> p k n", p=128)


        def mk_halves(banks, dts):
            out = {}
            for bk in banks:
                hs = []
                for hf in range(2):
                    t_ = half(bk, hf, dt=dts.get(bk, F32))
                    t_.b.r = list(ps[bk].b.r) + ([ps[bk].b.w] if ps[bk].b.w is not None else [])
                    hs.append(t_)
                out[bk] = hs
            return out

        def merge_halves(hv):
            for bk, hs in hv.items():
                for t_ in hs:
                    ps[bk].b.r.extend(t_.b.r)
                    if t_.b.w is not None:
                        ps[bk].b.r.append(t_.b.w)

        def stage_samp_attn(qT2, kTA, kTB, vtok, oaT):
            m = ar.mark()
            kcA = ar.alloc("kcA", [2, 1024], BF16)
            kcB = ar.alloc("kcB", [2, 1024], BF16)
            Vc = ar.alloc("Vc", [8, 256], BF16)
            Dh = ar.alloc("Dh", [1152], F32)
            tSs = ar.alloc("tSs", [1152], F32)
            Ps = ar.alloc("Ps", [1152], BF16)
            PTs = ar.alloc("PTs", [9, 128], BF16)
            ct = [ar.alloc("ct%d" % i, [2, 256], BF16) for i in range(2)]
            ots = ar.alloc("ots", [1024], BF16)
            sm = ar.alloc("ssm", [8], F32)
            for hb in range(2):
                A("sp", lambda e, hb=hb: e.dma_start(out=Dh.ap([(0, 1024)]), in_=tb["dsamp"][:, hb * 1024:(hb + 1) * 1024]), w=[Dh], lane="sa_d0")
                A("sp", lambda e: e.dma_start(out=Dh.ap([(1024, 128)]), in_=tb["dsamp"][:, 2048:2176]), w=[Dh], lane="sa_d1")
                for bl in range(8):
                    b = hb * 8 + bl
                    c_ = ct[bl % 2]
                    A("pool", lambda e, c_=c_, b=b: e.dma_start(out=c_.ap([0]), in_=ck[b]), w=[c_], lane="sa_ck%d" % (bl % 2))
                    A("pool", lambda e, bl=bl, b=b: e.dma_start(out=Vc.ap([bl]), in_=cv[b]), w=[Vc], lane="sa_cv")
                    A("dve", lambda e, c_=c_: e.tensor_copy(out=bass.AP(ar.h16, c_.off // 2 + 256, [[ar.h16.shape[1], 128], [128, 2], [1, 64]]),
                                                            in_=bass.AP(ar.h16, c_.off // 2 + 64, [[ar.h16.shape[1], 128], [128, 2], [1, 64]])), r=[c_], w=[c_])
                    A("dve", lambda e, c_=c_: e.tensor_copy(out=bass.AP(ar.h16, c_.off // 2 + 256 + 64, [[ar.h16.shape[1], 128], [128, 2], [1, 64]]),
                                                            in_=bass.AP(ar.h16, c_.off // 2, [[ar.h16.shape[1], 128], [128, 2], [1, 64]])), r=[c_], w=[c_])
                    tb_ = 4 + bl % 2
                    for v_ in range(2):
                        for m_ in range(2):
                            A("pe", lambda e, c_=c_, v_=v_, m_=m_, tb_=tb_: e.transpose(ps16[tb_].ap([((v_ * 2 + m_) * 128, 128)]), c_.ap([v_, (m_ * 128, 128)]), identb.ap()),
                              r=[c_, identb], w=[ps[tb_]])
                    A("act", lambda e, bl=bl, tb_=tb_: e.copy(out=kcA.ap([None, (bl * 128, 128)]), in_=bass.AP(psk[tb_].h16, 0, [[1024, 128], [128, 2], [1, 128]])),
                      r=[ps[tb_]], w=[kcA])
                    A("act", lambda e, bl=bl, tb_=tb_: e.copy(out=kcB.ap([None, (bl * 128, 128)]), in_=bass.AP(psk[tb_].h16, 256, [[1024, 128], [128, 2], [1, 128]])),
                      r=[ps[tb_]], w=[kcB])
                for h in range(16):
                    e_, g_ = h % 2, h // 4
                    useA = (g_ % 2) == e_
                    kc, kt, kp, base = (kcA if useA else kcB), (kTA if useA else kTB), g_ // 2, 64 * e_
                    q_ap = lambda: qT2.ap([h // 2, (1024, 128)], p0=base, pn=64)
                    for j_ in range(2):
                        A("pe", lambda e, j_=j_, kc=kc, kp=kp, base=base, h=h: e.matmul(ps[j_].ap(), lhsT=qT2.ap([h // 2, (1024, 128)], p0=base, pn=64),
                                                                                       rhs=kc.ap([kp, (j_ * 512, 512)], p0=base, pn=64), start=True, stop=True),
                          r=[qT2, kc], w=[ps[j_]])
                    A("pe", lambda e, kt=kt, kp=kp, base=base, h=h: e.matmul(ps[2].ap([(0, 128)]), lhsT=qT2.ap([h // 2, (1024, 128)], p0=base, pn=64),
                                                                             rhs=kt.ap([kp, (1152, 128)], p0=base, pn=64), start=True, stop=True),
                      r=[qT2, kt], w=[ps[2]])
                    for j_, (c0, cw) in enumerate([(0, 512), (512, 512), (1024, 128)]):
                        A("dve", lambda e, j_=j_, c0=c0, cw=cw, h=h: e.scalar_tensor_tensor(out=tSs.ap([(c0, cw)]), in0=Dh.ap([(c0, cw)]), scalar=-slopes[h],
                                                                                          in1=ps[j_].ap([(0, cw)]), op0=ALU.mult, op1=ALU.add),
                          r=[Dh, ps[j_]], w=[tSs])
                    A("dve", lambda e: e.tensor_reduce(out=sm.ap([(0, 1)]), in_=tSs.ap(), axis=AX.X, op=ALU.max), r=[tSs], w=[sm])
                    A("dve", lambda e, h=h: e.tensor_tensor(out=sm.ap([(0, 1)]), in0=sm.ap([(0, 1)]), in1=sinks.ap([(h, 1)]), op=ALU.max), r=[sm, sinks], w=[sm])
                    A("dve", lambda e: e.tensor_scalar(out=sm.ap([(1, 1)]), in0=sm.ap([(0, 1)]), scalar1=-1.0, scalar2=None, op0=ALU.mult), r=[sm], w=[sm])
                    A("act", lambda e: e.activation(out=Ps.ap(), in_=tSs.ap(), func=AF.Exp, bias=sm.ap([(1, 1)]), scale=1.0, accum_out=sm.ap([(2, 1)])),
                      r=[tSs, sm], w=[Ps, sm])
                    A("dve", lambda e, h=h: e.tensor_tensor(out=sm.ap([(3, 1)]), in0=sm.ap([(1, 1)]), in1=sinks.ap([(h, 1)]), op=ALU.add), r=[sm, sinks], w=[sm])
                    A("act", lambda e: e.activation(out=sm.ap([(3, 1)]), in_=sm.ap([(3, 1)]), func=AF.Exp), r=[sm], w=[sm])
                    A("dve", lambda e: e.tensor_tensor(out=sm.ap([(4, 1)]), in0=sm.ap([(2, 1)]), in1=sm.ap([(3, 1)]), op=ALU.add), r=[sm], w=[sm])
                    A("dve", lambda e: e.reciprocal(out=sm.ap([(5, 1)]), in_=sm.ap([(4, 1)])), r=[sm], w=[sm])
                    A("pool", lambda e: e.tensor_scalar(out=Ps.ap(), in0=Ps.ap(), scalar1=sm.ap([(5, 1)]), scalar2=None, op0=ALU.mult), r=[Ps, sm], w=[Ps])
                    for kt_ in range(9):
                        bk_, off_ = (4, kt_ * 128) if kt_ < 8 else (5, 0)
                        A("pe", lambda e, kt_=kt_, bk_=bk_, off_=off_: e.transpose(ps16[bk_].ap([(off_, 128)]), Ps.ap([(kt_ * 128, 128)]), identb.ap()),
                          r=[Ps, identb], w=[ps[bk_]])
                    A("act", lambda e: e.copy(out=PTs.ap([(0, 8)]), in_=ps16[4].ap()), r=[ps[4]], w=[PTs])
                    A("act", lambda e: e.copy(out=PTs.ap([8]), in_=ps16[5].ap([(0, 128)])), r=[ps[5]], w=[PTs])
                    for kt_ in range(9):
                        rhs_t, rhs_i = (Vc, kt_) if kt_ < 8 else (vtok, 9)
                        A("pe", lambda e, kt_=kt_, rhs_t=rhs_t, rhs_i=rhs_i, g_=g_: e.matmul(ps[6].ap([(0, 64)]), lhsT=PTs.ap([kt_]), rhs=rhs_t.ap([rhs_i, (g_ * 64, 64)]),
                                                                                           start=(kt_ == 0), stop=(kt_ == 8)),
                          r=[PTs, rhs_t], w=[ps[6]])
                    A("act", lambda e, h=h, hb=hb: e.copy(out=ots.ap([(h * 64, 64)], p0=64 * hb, pn=64), in_=ps[6].ap([(0, 64)], p0=64 * hb, pn=64)), r=[ps[6]], w=[ots])
            for m_ in range(8):
                A("pe", lambda e, m_=m_: e.transpose(ps16[4].ap([(m_ * 128, 128)]), ots.ap([(m_ * 128, 128)]), identb.ap()), r=[ots, identb], w=[ps[4]])
            A("dve", lambda e: e.tensor_copy(out=oaT.ap([None, (1024, 128)]), in_=bass.AP(psk[4].h16, 0, [[1024, 128], [128, 8], [1, 128]])), r=[ps[4]], w=[oaT])
            ar.release(m)


        def stage_ret(p, NT, T, sub, orT, hT, hTb):
            S.stage_begin()
            samp = NT == 9
            last = (p == npass - 1)
            m = ar.mark()
            wr = [ar.alloc("rw%d" % i, [16, 256], BF16) for i in range(3)]
            rc = [ar.alloc("rc%d" % i, [256], F32) for i in range(2)]
            rsn = [ar.alloc("rsn%d" % i, [256], F32) for i in range(2)]
            dec = [ar.alloc("dec%d" % i, [128], F32) for i in range(2)]
            qw = [ar.alloc("qw%d" % i, [128], F32) for i in range(2)]
            kw = [ar.alloc("kw%d" % i, [8], F32) for i in range(2)]
            A("sp", lambda e: e.dma_start(out=kw[0].ap(), in_=tb["kw_p"]), w=[kw[0]], lane="r_kw0")
            A("sp", lambda e: e.dma_start(out=kw[1].ap(), in_=tb["kw_s"]), w=[kw[1]], lane="r_kw1")
            if samp:
                Sbb = ar.alloc("Sbb", [16, 256], BF16)
                qz = ar.alloc("qz", [16, 128], BF16)
                kzm = ar.alloc("kzm", [16, 128], BF16)
                sf = [ar.alloc("sf%d" % i, [512], F32) for i in range(2)]
                sn = [ar.alloc("sn%d" % i, [512], F32) for i in range(2)]
                bm = ar.alloc("bm", [16], F32)
                A("sp", lambda e: e.dma_start(out=bm.ap(), in_=tb["bmask"]), w=[bm], lane="r_bm")
                A("dve", lambda e: e.memset(qz.ap(), 0.0), w=[qz])
            R = 4
            qkr = [sub.alloc("qkr%d" % i, [256], BF16) for i in range(R)]
            kz = [sub.alloc("kz%d" % i, [128], BF16) for i in range(R)]
            vt = [sub.alloc("vt%d" % i, [256], BF16) for i in range(R)]
            sgt = [sub.alloc("sgt%d" % i, [256], BF16) for i in range(R)]
            qkT = [sub.alloc("qkT%d" % i, [256], BF16) for i in range(R)]
            qwT = [sub.alloc("qwT%d" % i, [128], BF16) for i in range(R)]
            sT = [sub.alloc("sT%d" % i, [128], BF16) for i in range(2)]
            ot = [sub.alloc("rot%d" % i, [256], BF16) for i in range(2)]
            ra = [sub.alloc("ra%d" % i, [256], F32) for i in range(2)]
            rb = [sub.alloc("rb%d" % i, [256], F32) for i in range(2)]
            Sbf = [sub.alloc("Sbf%d" % i, [256], BF16) for i in range(2)]
            ysm = [sub.alloc("ysm%d" % i, [4], F32) for i in range(2)]
            def hb_(bank, hf, dt=F32):
                t_ = half(bank, hf, dt=dt)
                t_.b = ps[bank].b
                return t_
            hv = {2: [hb_(2, 0)], 3: [hb_(3, 0)], 6: [hb_(6, 0), hb_(6, 1)], 7: [hb_(7, 0), hb_(7, 1)]}
            hBt = hb_(4, 0, BF16)
            hCt = hb_(4, 1, F32)
            hE = [hb_(5, 0, BF16), hb_(5, 1, BF16)]
            scnt = [0]

            def do_head(h):
                wqk, wv, wg = wr[(3 * h) % 3], wr[(3 * h + 1) % 3], wr[(3 * h + 2) % 3]
                A("pool", lambda e, wqk=wqk, h=h: e.dma_start(out=wqk.ap([None, (0, 128)]), in_=wslice(w_in, C_QR + h * 128, 128)), w=[wqk], lane="r_w0")
                A("pool", lambda e, wqk=wqk, h=h: e.dma_start(out=wqk.ap([None, (128, 128)]), in_=wslice(w_in, C_KR + h * 128, 128)), w=[wqk], lane="r_w0b")
                A("pool", lambda e, wv=wv, h=h: e.dma_start(out=wv.ap(), in_=wslice(w_in, C_VR + h * 256, 256)), w=[wv], lane="r_w1")
                A("pool", lambda e, wg=wg, h=h: e.dma_start(out=wg.ap(), in_=wslice(w_in, C_GR + h * 256, 256)), w=[wg], lane="r_w2")
                A("sp", lambda e, h=h: e.dma_start(out=dec[0].ap(), in_=tb["decT_p"][:, h, :]), w=[dec[0]], lane="r_d0")
                A("sp", lambda e, h=h: e.dma_start(out=qw[0].ap(), in_=tb["qw_p"][:, h, :]), w=[qw[0]], lane="r_q0")
                if samp:
                    A("sp", lambda e, h=h: e.dma_start(out=dec[1].ap(), in_=tb["decT_s"][:, h, :]), w=[dec[1]], lane="r_d1")
                    A("sp", lambda e, h=h: e.dma_start(out=qw[1].ap(), in_=tb["qw_s"][:, h, :]), w=[qw[1]], lane="r_q1")
                    A("pool", lambda e, h=h: e.dma_start(out=Sbb.ap(), in_=st_in[:, h].rearrange("b p n -> p b n")), w=[Sbb], lane="r_sb")
                sb0 = Sbf[scnt[0] % 2]
                A("act", lambda e: e.copy(out=sb0.ap(), in_=Scar.ap([h])), r=[Scar], w=[sb0])

                def stepA(t):
                    bankA, hg = ps[t % 2], hv[2 + t % 2][0]
                    for k in range(KD):
                        A("pe", lambda e, k=k: e.matmul(bankA.ap([(0, 256)]), lhsT=hT.ap([k, (t * 128, 128)]), rhs=wqk.ap([k]), start=(k == 0), stop=(k == KD - 1)),
                          r=[hTb[min(t // 4, 2)], wqk], w=[bankA])
                    for k in range(KD):
                        A("pe", lambda e, k=k: e.matmul(bankA.ap([(256, 256)]), lhsT=hT.ap([k, (t * 128, 128)]), rhs=wv.ap([k]), start=(k == 0), stop=(k == KD - 1)),
                          r=[hTb[min(t // 4, 2)], wv], w=[bankA])
                    for k in range(KD):
                        A("pe", lambda e, k=k: e.matmul(hg.ap(), lhsT=hT.ap([k, (t * 128, 128)]), rhs=wg.ap([k]), start=(k == 0), stop=(k == KD - 1)),
                          r=[hTb[min(t // 4, 2)], wg], w=[hg])
                    ti = (p * 8 + t) if t < 8 else 32
                    rc_, rs_, ra_, rb_ = rc[t % 2], rsn[t % 2], ra[t % 2], rb[t % 2]
                    A("sp", lambda e: e.dma_start(out=rc_.ap(), in_=tb["rotc"][:, ti, :]), w=[rc_], lane="r_rc%d" % (t % 2))
                    A("sp", lambda e: e.dma_start(out=rs_.ap(), in_=tb["rots"][:, ti, :]), w=[rs_], lane="r_rs%d" % (t % 2))
                    x_ = bankA
                    A("dve", lambda e: e.tensor_tensor(out=ra_.ap(), in0=x_.ap([(0, 256)]), in1=rc_.ap(), op=ALU.mult), r=[x_, rc_], w=[ra_])
                    def hv4(tile_, base_off, es, hsel, row):
                        return bass.AP(tile_.ar.h32 if es == 4 else tile_.ar.h16, tile_.off // es + base_off + hsel * 64, [[row, 128], [128, 2], [1, 64]])
                    A("dve", lambda e: e.tensor_tensor(out=hv4(rb_, 0, 4, 0, ar.h32.shape[1]), in0=hv4(x_, 0, 4, 1, 512), in1=hv4(rs_, 0, 4, 0, ar.h32.shape[1]), op=ALU.mult),
                      r=[x_, rs_], w=[rb_])
                    A("dve", lambda e: e.tensor_tensor(out=hv4(rb_, 0, 4, 1, ar.h32.shape[1]), in0=hv4(x_, 0, 4, 0, 512), in1=hv4(rs_, 0, 4, 1, ar.h32.shape[1]), op=ALU.mult),
                      r=[x_, rs_], w=[rb_])
                    q_ = qkr[t % R]
                    A("pool", lambda e: e.tensor_tensor(out=q_.ap(), in0=ra_.ap(), in1=rb_.ap(), op=ALU.add), r=[ra_, rb_], w=[q_])
                    kwt = kw[1] if t == 8 else kw[0]
                    A("pool", lambda e: e.tensor_scalar(out=kz[t % R].ap(), in0=q_.ap([(128, 128)]), scalar1=kwt.ap([(h, 1)]), scalar2=None, op0=ALU.mult),
                      r=[q_, kwt], w=[kz[t % R]])
                    A("act", lambda e: e.copy(out=vt[t % R].ap(), in_=x_.ap([(256, 256)])), r=[x_], w=[vt[t % R]])
                    A("act", lambda e: e.activation(out=sgt[t % R].ap(), in_=hg.ap(), func=AF.Silu), r=[hg], w=[sgt[t % R]])

                def stepB(t):
                    q_, hB = qkr[t % R], hBt
                    for j_ in range(2):
                        A("pe", lambda e, j_=j_: e.transpose(hB.ap([(j_ * 128, 128)]), q_.ap([(j_ * 128, 128)]), identb.ap()), r=[q_, identb], w=[hB])
                    A("act", lambda e: e.copy(out=qkT[t % R].ap(), in_=hB.ap([(0, 256)])), r=[hB], w=[qkT[t % R]])
                    qwt = qw[1] if t == 8 else qw[0]
                    A("dve", lambda e: e.tensor_tensor(out=qwT[t % R].ap(), in0=qkT[t % R].ap([(0, 128)]), in1=qwt.ap(), op=ALU.mult), r=[qkT[t % R], qwt], w=[qwT[t % R]])
                    hC = hCt
                    A("pe", lambda e: e.matmul(hC.ap([(0, 128)]), lhsT=qkT[t % R].ap([(128, 128)]), rhs=qkT[t % R].ap([(0, 128)]), start=True, stop=True),
                      r=[qkT[t % R]], w=[hC])
                    dct = dec[1] if t == 8 else dec[0]
                    A("dve", lambda e: e.tensor_tensor(out=sT[t % 2].ap(), in0=hC.ap([(0, 128)]), in1=dct.ap(), op=ALU.mult), r=[hC, dct], w=[sT[t % 2]])

                def stepD(t):
                    hY, hK = hv[6][t % 2], hv[7][t % 2]
                    if t < 8:
                        sb_ = Sbf[scnt[0] % 2]
                        A("pe", lambda e: e.matmul(hY.ap(), lhsT=sT[t % 2].ap(), rhs=vt[t % R].ap(), start=True, stop=False), r=[sT[t % 2], vt[t % R]], w=[hY])
                        A("pe", lambda e: e.matmul(hY.ap(), lhsT=qwT[t % R].ap(), rhs=sb_.ap(), start=False, stop=True), r=[qwT[t % R], sb_], w=[hY])
                        A("pe", lambda e: e.matmul(hK.ap(), lhsT=kz[t % R].ap(), rhs=vt[t % R].ap(), start=True, stop=True), r=[kz[t % R], vt[t % R]], w=[hK])
                        A("dve", lambda e: e.scalar_tensor_tensor(out=Scar.ap([h]), in0=Scar.ap([h]), scalar=g128[h], in1=hK.ap(), op0=ALU.mult, op1=ALU.add),
                          r=[Scar, hK], w=[Scar])
                        scnt[0] += 1
                        nb_ = Sbf[scnt[0] % 2]
                        A("act", lambda e: e.copy(out=nb_.ap(), in_=Scar.ap([h])), r=[Scar], w=[nb_])
                    else:
                        A("dve", lambda e: e.tensor_tensor(out=bass.AP(ar.h16, qz.off // 2, [[ar.h16.shape[1], 128], [136, 16], [1, 8]]),
                                                           in0=qkT[t % R].ap([(0, 128)]), in1=qw[1].ap(), op=ALU.mult) if False else
                          e.tensor_copy(out=bass.AP(ar.h16, qz.off // 2, [[ar.h16.shape[1], 128], [136, 16], [1, 8]]),
                                        in_=bass.AP(sub.h16, qwT[t % R].off // 2, [[sub.h16.shape[1], 128], [8, 16], [1, 8]])),
                          r=[qwT[t % R]], w=[qz])
                        A("dve", lambda e: e.tensor_tensor(out=kzm.ap(merge=False), in0=bass.AP(sub.h16, kz[t % R].off // 2, [[sub.h16.shape[1], 128], [0, 16], [1, 128]]),
                                                           in1=bass.AP(ar.h32, bm.off // 4, [[ar.h32.shape[1], 128], [1, 16], [0, 128]]), op=ALU.mult),
                          r=[kz[t % R], bm], w=[kzm])
                        A("pe", lambda e: e.matmul(hY.ap(), lhsT=sT[t % 2].ap(), rhs=vt[t % R].ap(), start=True, stop=False), r=[sT[t % 2], vt[t % R]], w=[hY])
                        for b in range(NB):
                            A("pe", lambda e, b=b: e.matmul(hY.ap(), lhsT=qz.ap([b]), rhs=Sbb.ap([b]), start=False, stop=(b == NB - 1)), r=[qz, Sbb], w=[hY])
                        for bp in range(NB // 2):
                            sf_, sn_ = sf[bp % 2], sn[bp % 2]
                            A("sp", lambda e, bp=bp, sf_=sf_: e.dma_start(out=sf_.ap(), in_=st_in[2 * bp:2 * bp + 2, h].rearrange("b p n -> p b n")), w=[sf_], lane="r_sf%d" % (bp % 2))
                            for j_ in range(2):
                                hk_ = hv[7][j_]
                                A("pe", lambda e, bp=bp, j_=j_, hk_=hk_: e.matmul(hk_.ap(), lhsT=kzm.ap([2 * bp + j_]), rhs=vt[t % R].ap(), start=True, stop=True),
                                  r=[kzm, vt[t % R]], w=[hk_])
                                A("dve", lambda e, j_=j_, hk_=hk_, sf_=sf_, sn_=sn_: e.scalar_tensor_tensor(out=sn_.ap([(j_ * 256, 256)]), in0=sf_.ap([(j_ * 256, 256)]), scalar=g8[h],
                                                                                                      in1=hk_.ap(), op0=ALU.mult, op1=ALU.add),
                                  r=[sf_, hk_], w=[sn_])
                            A("sp", lambda e, bp=bp, sn_=sn_: e.dma_start(out=srs[2 * bp:2 * bp + 2, h].rearrange("b p n -> p b n"), in_=sn_.ap()), r=[sn_], lane="r_sn%d" % (bp % 2))
                    ys = ysm[t % 2]
                    o_ = ot[t % 2]
                    A("act", lambda e: e.activation(out=ra[t % 2].ap(), in_=hY.ap(), func=AF.Square, accum_out=ys.ap([(0, 1)])), r=[hY], w=[ra[t % 2], ys])
                    A("act", lambda e: e.activation(out=ys.ap([(1, 1)]), in_=ys.ap([(0, 1)]), func=AF.Sqrt, scale=1.0 / 256, bias=cst.ap([(0, 1)])), r=[ys, cst], w=[ys])
                    A("dve", lambda e: e.reciprocal(out=ys.ap([(2, 1)]), in_=ys.ap([(1, 1)])), r=[ys], w=[ys])
                    A("dve", lambda e: e.scalar_tensor_tensor(out=o_.ap(), in0=hY.ap(), scalar=ys.ap([(2, 1)]), in1=sgt[t % R].ap(), op0=ALU.mult, op1=ALU.mult),
                      r=[hY, ys, sgt[t % R]], w=[o_])
                    he_ = hE[t % 2]
                    for j_ in range(2):
                        A("pe", lambda e, j_=j_: e.transpose(he_.ap([(j_ * 128, 128)]), o_.ap([(j_ * 128, 128)]), identb.ap()), r=[o_, identb], w=[he_])
                    A("act", lambda e: e.copy(out=orT.ap([(2 * h, 2), (t * 128, 128)]), in_=bass.AP(psk[5].h16, (t % 2) * 512, [[1024, 128], [128, 2], [1, 128]])),
                      r=[he_], w=[orT])

                stepA(0)
                for t in range(NT):
                    if t + 1 < NT:
                        stepA(t + 1)
                    stepB(t)
                    stepD(t)
            for h_ in range(8):
                do_head(h_)
            if last:
                A("sp", lambda e: e.dma_start(out=srp.rearrange("h p n -> p h n"), in_=Scar.ap()), r=[Scar], lane="o_srp")
            ar.release(m)

        def stage_merge(T, groups, sub, oaT, orT, hT, hTb, acc, accb):
            S.stage_begin()
            m = ar.mark()
            wpa = [ar.alloc("wpa%d" % i, [8, 128], BF16) for i in range(2)]
            wpr = [ar.alloc("wpr%d" % i, [16, 128], BF16) for i in range(2)]
            wga = [ar.alloc("wga%d" % i, [16, 128], BF16) for i in range(2)]
            wgr = [ar.alloc("wgr%d" % i, [16, 128], BF16) for i in range(2)]
            sa = [ar.alloc("msa%d" % i, [512], F32) for i in range(2)]
            sr = [ar.alloc("msr%d" % i, [512], F32) for i in range(2)]
            m1 = [ar.alloc("mm1%d" % i, [512], F32) for i in range(2)]
            m2 = [ar.alloc("mm2%d" % i, [512], F32) for i in range(2)]
            mt = [ar.alloc("mt%d" % i, [1152], BF16) for i in range(2)]
            cnt = 0
            for c in range(KD):
                j = c % 2
                A("pool", lambda e, c=c, j=j: e.dma_start(out=wpa[j].ap(), in_=wslice(w_pa, c * 128, 128)), w=[wpa[j]], lane="g_pa%d" % j)
                A("pool", lambda e, c=c, j=j: e.dma_start(out=wpr[j].ap(), in_=wslice(w_pr, c * 128, 128)), w=[wpr[j]], lane="g_pr%d" % j)
                A("pool", lambda e, c=c, j=j: e.dma_start(out=wga[j].ap(), in_=wslice(w_in, C_GA + c * 128, 128)), w=[wga[j]], lane="g_ga%d" % j)
                A("pool", lambda e, c=c, j=j: e.dma_start(out=wgr[j].ap(), in_=wslice(w_in, C_GRR + c * 128, 128)), w=[wgr[j]], lane="g_gr%d" % j)
                for g, (g0, gw) in enumerate(groups):
                    b0 = (cnt % 2) * 4
                    q = cnt % 2
                    cnt += 1
                    for k in range(8):
                        A("pe", lambda e, k=k, j=j, b0=b0, g0=g0, gw=gw: e.matmul(ps[b0].ap([(0, gw)]), lhsT=wpa[j].ap([k]), rhs=oaT.ap([k, (g0, gw)]), start=(k == 0), stop=(k == 7)),
                          r=[wpa[j], oaT], w=[ps[b0]])
                    for k in range(KD):
                        A("pe", lambda e, k=k, j=j, b0=b0, g0=g0, gw=gw: e.matmul(ps[b0 + 1].ap([(0, gw)]), lhsT=wpr[j].ap([k]), rhs=orT.ap([k, (g0, gw)]), start=(k == 0), stop=(k == KD - 1)),
                          r=[wpr[j], orT], w=[ps[b0 + 1]])
                    for k in range(KD):
                        A("pe", lambda e, k=k, j=j, b0=b0, g0=g0, gw=gw: e.matmul(ps[b0 + 2].ap([(0, gw)]), lhsT=wga[j].ap([k]), rhs=hT.ap([k, (g0, gw)]), start=(k == 0), stop=(k == KD - 1)),
                          r=[wga[j], hTb[g]], w=[ps[b0 + 2]])
                    for k in range(KD):
                        A("pe", lambda e, k=k, j=j, b0=b0, g0=g0, gw=gw: e.matmul(ps[b0 + 3].ap([(0, gw)]), lhsT=wgr[j].ap([k]), rhs=hT.ap([k, (g0, gw)]), start=(k == 0), stop=(k == KD - 1)),
                          r=[wgr[j], hTb[g]], w=[ps[b0 + 3]])
                    A("act", lambda e, q=q, b0=b0, gw=gw: e.activation(out=sa[q].ap([(0, gw)]), in_=ps[b0 + 2].ap([(0, gw)]), func=AF.Sigmoid), r=[ps[b0 + 2]], w=[sa[q]])
                    A("act", lambda e, q=q, b0=b0, gw=gw: e.activation(out=sr[q].ap([(0, gw)]), in_=ps[b0 + 3].ap([(0, gw)]), func=AF.Sigmoid), r=[ps[b0 + 3]], w=[sr[q]])
                    A("dve", lambda e, q=q, b0=b0, gw=gw: e.tensor_tensor(out=m1[q].ap([(0, gw)]), in0=sa[q].ap([(0, gw)]), in1=ps[b0].ap([(0, gw)]), op=ALU.mult), r=[sa[q], ps[b0]], w=[m1[q]])
                    A("dve", lambda e, q=q, b0=b0, gw=gw: e.tensor_tensor(out=m2[q].ap([(0, gw)]), in0=sr[q].ap([(0, gw)]), in1=ps[b0 + 1].ap([(0, gw)]), op=ALU.mult), r=[sr[q], ps[b0 + 1]], w=[m2[q]])
                    A("pool", lambda e, q=q, j=j, g0=g0, gw=gw: e.tensor_tensor(out=mt[j].ap([(g0, gw)]), in0=m1[q].ap([(0, gw)]), in1=m2[q].ap([(0, gw)]), op=ALU.add), r=[m1[q], m2[q]], w=[mt[j]])
                A("sp", lambda e, c=c, j=j: e.dma_start(out=mS[c * 128:(c + 1) * 128, 0:T], in_=mt[j].ap([(0, T)])), r=[mt[j]], w=[mSb[c]], lane="g_ms%d" % j)
            ar.release(m)
            for k in range(KD):
                A("sp", lambda e, k=k: e.dma_start(out=hT.ap([k, (0, T)]), in_=mS[k * 128:(k + 1) * 128, 0:T]), r=[mSb[k]], w=[hTb[g] for g in range(len(groups))], lane="g_ml%d" % (k % 4))
            inh = []
            for t_ in sub.all_tiles:
                for b_ in [t_.b] + t_.extra:
                    inh.extend(b_.r)
                    if b_.w is not None:
                        inh.append(b_.w)
            for row in accb:
                for b_ in row:
                    b_.r.extend(inh)
            m = ar.mark()
            wo = [ar.alloc("wo%d" % i, [16, 128], BF16) for i in range(3)]
            cnt = 0
            for c in range(KD):
                w_ = wo[c % 3]
                A("pool", lambda e, c=c, w_=w_: e.dma_start(out=w_.ap(), in_=wslice(w_o, c * 128, 128)), w=[w_], lane="g_wo%d" % (c % 3))
                for g, (g0, gw) in enumerate(groups):
                    bk = cnt % 4
                    cnt += 1
                    for k in range(KD):
                        A("pe", lambda e, k=k, w_=w_, bk=bk, g0=g0, gw=gw: e.matmul(ps[bk].ap([(0, gw)]), lhsT=w_.ap([k]), rhs=hT.ap([k, (g0, gw)]), start=(k == 0), stop=(k == KD - 1)),
                          r=[w_, hTb[g]], w=[ps[bk]])
                    if cnt % 2 == 0:
                        A("act", lambda e, c=c, bk=bk, g0=g0, gw=gw: e.copy(out=acc.ap([c, (g0, gw)]), in_=ps[bk].ap([(0, gw)])), r=[ps[bk]], w=[accb[c][g]])
                    else:
                        A("dve", lambda e, c=c, bk=bk, g0=g0, gw=gw: e.tensor_copy(out=acc.ap([c, (g0, gw)]), in_=ps[bk].ap([(0, gw)])), r=[ps[bk]], w=[accb[c][g]])
            ar.release(m)

        def stage_mixer(p, NT, T, groups, acc, accb, hT, hTb):
            samp = NT == 9
            last = (p == npass - 1)
            S.stage_begin()
            stop_here("m_pre")
            sub = SubArena(ar, acc, [b_ for row in accb for b_ in row])
            orT = sub.alloc("orT", [16, 1152], BF16)
            oaT = sub.alloc("oaT", [8, 1152], BF16)
            sub_mark = sub.mark()
            m0 = ar.mark()

            qT2 = sub.alloc("qT2", [8, 1152], BF16)
            kTA = ar.alloc("kTA", [2, 1280], BF16)
            kTB = ar.alloc("kTB", [2, 1280], BF16)
            vtok = ar.alloc("vtok", [10, 256], BF16)
            mA = ar.mark()
            wr = [ar.alloc("awr%d" % i, [16, 512], BF16) for i in range(2)]
            kv32 = [ar.alloc("kv32_%d" % i, [512], F32) for i in range(2)]
            kbf = [ar.alloc("kbf%d" % i, [2, 256], BF16) for i in range(2)]
            qbf = [ar.alloc("qbf%d" % i, [512], BF16) for i in range(2)]
            A("dve", lambda e: e.tensor_copy(out=kTA.ap([None, (0, 128)]), in_=kTAh.ap(merge=False)), r=[kTAh], w=[kTA])
            A("dve", lambda e: e.tensor_copy(out=kTB.ap([None, (0, 128)]), in_=kTBh.ap(merge=False)), r=[kTBh], w=[kTB])
            A("dve", lambda e: e.tensor_copy(out=vtok.ap([0]), in_=vhalo.ap()), r=[vhalo], w=[vtok])
            w0 = wr[0]
            A("pool", lambda e: e.dma_start(out=w0.ap(), in_=wslice(w_in, C_KA, 512)), w=[w0], lane="a_w0")
            for i in range(NT):
                bank = i % 2
                for k in range(KD):
                    A("pe", lambda e, k=k, i=i, bank=bank: e.matmul(ps[bank].ap(), lhsT=hT.ap([k, (i * 128, 128)]), rhs=w0.ap([k]),
                                                                    start=(k == 0), stop=(k == KD - 1)),
                      r=[hTb[min(i // 4, 2)], w0], w=[ps[bank]])
                kvt, kb = kv32[i % 2], kbf[i % 2]
                A("act", lambda e, kvt=kvt, bank=bank: e.copy(out=kvt.ap(), in_=ps[bank].ap()), r=[ps[bank]], w=[kvt])
                A("dve", lambda e, kb=kb, bank=bank: e.tensor_copy(out=kb.ap([0]), in_=ps[bank].ap([(0, 256)])), r=[ps[bank]], w=[kb])
                A(SWAP_ENG, lambda e, kb=kb, kvt=kvt: e.tensor_copy(out=kb.ap([1, (0, 2, 128), None][0:2] + []) if False else
                                                                  bass.AP(ar.h16, kb.off // 2 + 256, [[ar.h16.shape[1], 128], [128, 2], [1, 64]]),
                                                                  in_=bass.AP(ar.h32, kvt.off // 4 + 64, [[ar.h32.shape[1], 128], [128, 2], [1, 64]])),
                  r=[kvt], w=[kb])
                A(SWAP_ENG, lambda e, kb=kb, kvt=kvt: e.tensor_copy(out=bass.AP(ar.h16, kb.off // 2 + 256 + 64, [[ar.h16.shape[1], 128], [128, 2], [1, 64]]),
                                                                  in_=bass.AP(ar.h32, kvt.off // 4, [[ar.h32.shape[1], 128], [128, 2], [1, 64]])),
                  r=[kvt], w=[kb])
                A("dve", lambda e, i=i, bank=bank: e.tensor_copy(out=vtok.ap([i + 1]), in_=ps[bank].ap([(256, 256)])), r=[ps[bank]], w=[vtok])
                tb_ = 2 + i % 2
                if 't' in MIXSKIP:
                    continue
                for v_ in range(2):
                    for m_ in range(2):
                        A("pe", lambda e, kb=kb, v_=v_, m_=m_, tb_=tb_: e.transpose(ps16[tb_].ap([((v_ * 2 + m_) * 128, 128)]), kb.ap([v_, (m_ * 128, 128)]), identb.ap()),
                          r=[kb, identb], w=[ps[tb_]])
                A("act", lambda e, i=i, tb_=tb_: e.copy(out=kTA.ap([None, (128 + i * 128, 128)]), in_=bass.AP(psk[tb_].h16, 0, [[1024, 128], [128, 2], [1, 128]])),
                  r=[ps[tb_]], w=[kTA])
                A("act", lambda e, i=i, tb_=tb_: e.copy(out=kTB.ap([None, (128 + i * 128, 128)]), in_=bass.AP(psk[tb_].h16, 256, [[1024, 128], [128, 2], [1, 128]])),
                  r=[ps[tb_]], w=[kTB])
                if last and i == 7 and 'o' not in MIXSKIP:
                    A("sp", lambda e, kvt=kvt: e.dma_start(out=kwp, in_=kvt.ap([(0, 256)])), r=[kvt], lane=newlane("o_kw"))
                    A("sp", lambda e, kvt=kvt: e.dma_start(out=vwp, in_=kvt.ap([(256, 256)])), r=[kvt], lane=newlane("o_vw"))
                if i == 8 and 'o' not in MIXSKIP:
                    for b in range(NB):
                        A("sp", lambda e, kvt=kvt, b=b: e.dma_start(out=kws[b, 120:128, :], in_=kvt.ap([(0, 256)], p0=8 * b, pn=8)), r=[kvt], lane="o_ks%d" % (b % 4))
                        A("sp", lambda e, kvt=kvt, b=b: e.dma_start(out=vws[b, 120:128, :], in_=kvt.ap([(256, 256)], p0=8 * b, pn=8)), r=[kvt], lane="o_vs%d" % (b % 4))
            stop_here("m_kv")
            if samp:
                for b in range(NB):
                    A("sp", lambda e, b=b: e.dma_start(out=kws[b, 0:120, :], in_=ck[b, 8:128, :]), lane="o_kc%d" % (b % 4))
                    A("sp", lambda e, b=b: e.dma_start(out=vws[b, 0:120, :], in_=cv[b, 8:128, :]), lane="o_vc%d" % (b % 4))
            for cq in range(2):
                wq = wr[(cq + 1) % 2]
                A("pool", lambda e, wq=wq, cq=cq: e.dma_start(out=wq.ap(), in_=wslice(w_in, C_QA + cq * 512, 512)), w=[wq], lane="a_w%d" % ((cq + 1) % 2))
                for i in range(NT):
                    bank = i % 2
                    for k in range(KD):
                        A("pe", lambda e, k=k, i=i, bank=bank, wq=wq: e.matmul(ps[bank].ap(), lhsT=hT.ap([k, (i * 128, 128)]), rhs=wq.ap([k]),
                                                                               start=(k == 0), stop=(k == KD - 1)),
                          r=[hTb[min(i // 4, 2)], wq], w=[ps[bank]])
                    qb = qbf[i % 2]
                    A("act", lambda e, qb=qb, bank=bank: e.activation(out=qb.ap(), in_=ps[bank].ap(), func=AF.Copy, scale=0.125), r=[ps[bank]], w=[qb])
                    tb_ = 2 + i % 2
                    for m_ in range(4):
                        A("pe", lambda e, qb=qb, m_=m_, tb_=tb_: e.transpose(ps16[tb_].ap([(m_ * 128, 128)]), qb.ap([(m_ * 128, 128)]), identb.ap()),
                          r=[qb, identb], w=[ps[tb_]])
                    A("dve", lambda e, i=i, cq=cq, tb_=tb_: e.tensor_copy(out=qT2.ap([(cq * 4, 4), (i * 128, 128)]),
                                                                         in_=bass.AP(psk[tb_].h16, 0, [[1024, 128], [128, 4], [1, 128]])),
                      r=[ps[tb_]], w=[qT2])
            ar.release(mA)
            stop_here("m_proj")

            def khead(h):
                e_, g_ = h % 2, h // 4
                t_ = kTA if (g_ % 2) == e_ else kTB
                return t_, g_ // 2, 64 * e_

            dgen = ar.alloc("dgen", [256], F32)
            dfir = ar.alloc("dfir", [256], F32)
            A("sp", lambda e: e.dma_start(out=dgen.ap(), in_=tb["dgen"]), w=[dgen], lane="a_dg")
            A("sp", lambda e: e.dma_start(out=dfir.ap(), in_=tb["dfirst"]), w=[dfir], lane="a_df")
            tS = [ar.alloc("tS%d" % i, [4, 256], F32) for i in range(2)]
            Pn = [ar.alloc("Pn%d" % i, [4, 256], BF16) for i in range(2)]
            PTt = [ar.alloc("PTt%d" % i, [8, 128], BF16) for i in range(2)]
            otok = [ar.alloc("otok%d" % i, [1024], BF16) for i in range(2)]
            sml = [ar.alloc("sml%d" % i, [6, 4], F32) for i in range(2)]
            cnt = 0
            for i in range(8):
                Dt = dfir if (p == 0 and i == 0) else dgen
                ot = otok[i % 2]
                for g_ in range(4):
                    tS_, Pn_, PT_, sm = tS[cnt % 2], Pn[cnt % 2], PTt[cnt % 2], sml[cnt % 2]
                    sb0, sb1 = (cnt % 2) * 2, (cnt % 2) * 2 + 1
                    cnt += 1
                    for hh in range(4):
                        h = g_ * 4 + hh
                        kt, kp, base = khead(h)
                        bank = sb0 if hh % 2 == 0 else sb1
                        A("pe", lambda e, h=h, kt=kt, kp=kp, base=base, bank=bank, hh=hh, i=i: e.matmul(
                            ps[bank].ap([((hh // 2) * 256, 256)]), lhsT=qT2.ap([h // 2, (i * 128, 128)], p0=base, pn=64),
                            rhs=kt.ap([kp, (i * 128, 256)], p0=base, pn=64), start=True, stop=True),
                          r=[qT2, kt], w=[ps[bank]])
                        A("dve", lambda e, h=h, hh=hh, bank=bank, tS_=tS_, Dt=Dt: e.scalar_tensor_tensor(
                            out=tS_.ap([hh]), in0=Dt.ap(), scalar=-slopes[h], in1=ps[bank].ap([((hh // 2) * 256, 256)]), op0=ALU.mult, op1=ALU.add),
                          r=[Dt, ps[bank]], w=[tS_])
                    A("dve", lambda e, tS_=tS_, sm=sm: e.tensor_reduce(out=sm.ap([0]), in_=tS_.ap(merge=False), axis=AX.X, op=ALU.max), r=[tS_], w=[sm])
                    A("dve", lambda e, sm=sm, g_=g_: e.tensor_tensor(out=sm.ap([0]), in0=sm.ap([0]), in1=sinks.ap([(g_ * 4, 4)]), op=ALU.max), r=[sm, sinks], w=[sm])
                    A("dve", lambda e, sm=sm: e.tensor_scalar(out=sm.ap([1]), in0=sm.ap([0]), scalar1=-1.0, scalar2=None, op0=ALU.mult), r=[sm], w=[sm])
                    for hh in range(4):
                        A("act", lambda e, hh=hh, tS_=tS_, Pn_=Pn_, sm=sm: e.activation(out=Pn_.ap([hh]), in_=tS_.ap([hh]), func=AF.Exp, bias=sm.ap([1, (hh, 1)]),
                                                                                       scale=1.0, accum_out=sm.ap([2, (hh, 1)])),
                          r=[tS_, sm], w=[Pn_, sm])
                    A("dve", lambda e, sm=sm, g_=g_: e.tensor_tensor(out=sm.ap([3]), in0=sm.ap([1]), in1=sinks.ap([(g_ * 4, 4)]), op=ALU.add), r=[sm, sinks], w=[sm])
                    A("act", lambda e, sm=sm: e.activation(out=sm.ap([3]), in_=sm.ap([3]), func=AF.Exp), r=[sm], w=[sm])
                    A("dve", lambda e, sm=sm: e.tensor_tensor(out=sm.ap([4]), in0=sm.ap([2]), in1=sm.ap([3]), op=ALU.add), r=[sm], w=[sm])
                    A("dve", lambda e, sm=sm: e.reciprocal(out=sm.ap([5]), in_=sm.ap([4])), r=[sm], w=[sm])
                    for hh in range(4):
                        A("pool", lambda e, hh=hh, Pn_=Pn_, sm=sm: e.tensor_scalar(out=Pn_.ap([hh]), in0=Pn_.ap([hh]), scalar1=sm.ap([5, (hh, 1)]), scalar2=None, op0=ALU.mult),
                          r=[Pn_, sm], w=[Pn_])
                    tb_ = 4 + (cnt % 2)
                    for hh in range(4):
                        for kt_ in range(2):
                            A("pe", lambda e, hh=hh, kt_=kt_, Pn_=Pn_, tb_=tb_: e.transpose(ps16[tb_].ap([((hh * 2 + kt_) * 128, 128)]), Pn_.ap([hh, (kt_ * 128, 128)]), identb.ap()),
                              r=[Pn_, identb], w=[ps[tb_]])
                    A("act", lambda e, PT_=PT_, tb_=tb_: e.copy(out=PT_.ap(), in_=ps16[tb_].ap()), r=[ps[tb_]], w=[PT_])
                    ob = 6 + (i % 2)
                    for hh in range(4):
                        h = g_ * 4 + hh
                        for kt_ in range(2):
                            A("pe", lambda e, hh=hh, kt_=kt_, PT_=PT_, ob=ob, g_=g_, i=i: e.matmul(
                                ps[ob].ap([((g_ % 2) * 256 + hh * 64, 64)]), lhsT=PT_.ap([hh * 2 + kt_]), rhs=vtok.ap([i + kt_, (g_ * 64, 64)]),
                                start=(kt_ == 0), stop=(kt_ == 1)),
                              r=[PT_, vtok], w=[ps[ob]])
                    if g_ % 2 == 1:
                        A("act", lambda e, ot=ot, ob=ob, g_=g_: e.copy(out=ot.ap([((g_ // 2) * 512, 512)]), in_=ps[ob].ap()), r=[ps[ob]], w=[ot])
                tb_ = 4 + (i % 2)
                for m_ in range(8):
                    A("pe", lambda e, ot=ot, m_=m_, tb_=tb_: e.transpose(ps16[tb_].ap([(m_ * 128, 128)]), ot.ap([(m_ * 128, 128)]), identb.ap()),
                      r=[ot, identb], w=[ps[tb_]])
                A("dve", lambda e, i=i, tb_=tb_: e.tensor_copy(out=oaT.ap([None, (i * 128, 128)]), in_=bass.AP(psk[tb_].h16, 0, [[1024, 128], [128, 8], [1, 128]])),
                  r=[ps[tb_]], w=[oaT])
            A("dve", lambda e: e.tensor_copy(out=kTAh.ap(merge=False), in_=kTA.ap([None, (1024, 128)])), r=[kTA], w=[kTAh])
            A("dve", lambda e: e.tensor_copy(out=kTBh.ap(merge=False), in_=kTB.ap([None, (1024, 128)])), r=[kTB], w=[kTBh])
            A("dve", lambda e: e.tensor_copy(out=vhalo.ap(), in_=vtok.ap([8])), r=[vtok], w=[vhalo])
            ar.release(mA)
            stop_here("m_attn")
            if samp:
                stage_samp_attn(qT2, kTA, kTB, vtok, oaT)
            stop_here("m_sattn")
            ar.release(m0)
            sub.release(sub_mark)
            stage_ret(p, NT, T, sub, orT, hT, hTb)
            stop_here("m_ret")
            ar.release(m0)
            sub.release(sub_mark)
            stage_merge(T, groups, sub, oaT, orT, hT, hTb, acc, accb)
            ar.release(m0)

        stage_mod()
        hT = ar.alloc("hT", [16, 1152], BF16)
        acc = ar.alloc("acc", [16, 1152], F32)
        hTb = [Buf() for _ in range(3)]
        for b_ in hTb:
            b_.r = list(hT.b.r)
        accb = [[Buf() for _ in range(3)] for _ in range(KD)]
        for k in range(KD):
            for g in range(3):
                accb[k][g].r = list(acc.b.r)
        base_mark = ar.mark()

        def dump(name, T):
            if name in dbg:
                for k in range(KD):
                    A("sp", lambda e, k=k: e.dma_start(out=dbg[name][k * 128:(k + 1) * 128, 0:T], in_=acc.ap([k, (0, T)])),
                      r=[accb[k][g] for g in range(3)], lane=newlane("dbg"))

        for p in range(npass):
          try:
            samp = (p == 0)
            NT = 9 if samp else 8
            T = NT * 128
            groups = [(0, 512), (512, 512)] + ([(1024, 128)] if samp else [])
            stage_p0(p, NT, T, groups, acc, accb, hT, hTb, xsA, xsAb)
            prenorm(0, p, T, groups, acc, accb, hT, hTb)
            stage_ffn(f1g, f1u, f1d, T, groups, acc, accb, hT, hTb)
            stage_post(0, T, groups, acc, accb, xsA, xsAb)
            if p == 0:
                dump("x1T", T)
            if stop_after == "post0":
                break
            stash_out(T, groups, acc, accb, xsB, xsBb)
            prenorm(1, p, T, groups, acc, accb, hT, hTb)
            stage_mixer(p, NT, T, groups, acc, accb, hT, hTb)
            stop_here("mixer")
            stage_post(1, T, groups, acc, accb, xsB, xsBb)
            stash_out(T, groups, acc, accb, xsA, xsAb)
            prenorm(2, p, T, groups, acc, accb, hT, hTb)
            stage_ffn(f2g, f2u, f2d, T, groups, acc, accb, hT, hTb)
            stage_post(2, T, groups, acc, accb, xsA, xsAb)
            stage_out(p, NT, T, groups, acc, accb)
          except _StopBuild:
            break
        print("arena high-water: %d / %d bytes" % (ar.hw, ar.size))
        S.emit()
    return nc


def _in_maps(inp):
    f = np.float32
    g = lambda k: np.asarray(inp[k], dtype=f)
    x_prompt, x_sample = g("x_prompt"), g("x_sample")
    c_prompt, c_sample = g("c_prompt"), g("c_sample")
    b_ada = g("b_ada")[0]
    badd = np.ascontiguousarray(np.broadcast_to(b_ada.reshape(144, 128).T[:, :, None], (128, 144, 17))).reshape(128, 144 * 17)
    norms = np.concatenate([g("norm_pre")[0], g("norm_post")[0]], 0)
    normsT = np.ascontiguousarray(norms.reshape(6, 16, 128).transpose(2, 0, 1)).reshape(128, 96)
    sinksR = np.ascontiguousarray(np.broadcast_to(g("attn_sinks")[0][None, :], (128, 16)))
    shared = {
        "w_ada": g("w_ada")[0], "badd": badd, "normsT": normsT, "sinksR": sinksR,
        "w_in": g("w_in")[0], "w_pa": g("w_pa")[0], "w_pr": g("w_pr")[0], "w_o": g("w_o")[0],
        "f1g": g("ffn1_gate")[0], "f1u": g("ffn1_up")[0], "f1d": g("ffn1_down")[0],
        "f2g": g("ffn2_gate")[0], "f2u": g("ffn2_up")[0], "f2d": g("ffn2_down")[0],
    }
    for k in ("identf", "rotc", "rots", "decT_p", "decT_s", "qw_p", "qw_s", "kw_p", "kw_s", "dgen", "dfirst", "dsamp", "bmask"):
        shared["t_" + k] = TAB[k]
    ck, cv, stt = g("cache_k_win")[0], g("cache_v_win")[0], g("state_ret")[0]
    maps = []
    for c in range(8):
        bsl = slice(c * NB, (c + 1) * NB)
        m = dict(shared)
        m["xp"] = x_prompt[c % 2]
        m["xs"] = np.ascontiguousarray(x_sample[bsl].reshape(NB * 8, D))
        cc = np.concatenate([c_prompt[c % 2][None, :], c_sample[bsl]], 0)
        m["cT"] = np.ascontiguousarray(cc.T)
        m["ck"] = np.ascontiguousarray(ck[bsl].reshape(NB, 128, 256))
        m["cv"] = np.ascontiguousarray(cv[bsl].reshape(NB, 128, 256))
        m["st"] = np.ascontiguousarray(stt[bsl])
        maps.append(m)
    return maps


_NC_CACHE = {}


def kernel(**inputs):
    if "nc" not in _NC_CACHE:
        _NC_CACHE["nc"] = build()
    nc = _NC_CACHE["nc"]
    maps = _in_maps(inputs)
    res = run_bass_kernel_spmd(nc, maps, core_ids=list(range(8)))
    R = res.results
    f = np.float32
    yp = np.stack([R[0]["yp"], R[1]["yp"]], 0).astype(f)
    ys = np.concatenate([R[c]["ys"].reshape(NB, 8, D) for c in range(8)], 0).astype(f)
    kwp = np.stack([R[0]["kwp"], R[1]["kwp"]], 0).reshape(1, 2, 128, 4, 64).astype(f)
    vwp = np.stack([R[0]["vwp"], R[1]["vwp"]], 0).reshape(1, 2, 128, 4, 64).astype(f)
    srp = np.stack([R[0]["srp"], R[1]["srp"]], 0).reshape(1, 2, 8, 128, 256).astype(f)
    kws = np.concatenate([R[c]["kws"] for c in range(8)], 0).reshape(1, 128, 128, 4, 64).astype(f)
    vws = np.concatenate([R[c]["vws"] for c in range(8)], 0).reshape(1, 128, 128, 4, 64).astype(f)
    srs = np.concatenate([R[c]["srs"] for c in range(8)], 0).reshape(1, 128, 8, 128, 256).astype(f)
    return (yp, ys, kwp, vwp, srp, kws, vws, srs)
```
